# Optimizing a Trainium2 kernel written in Bass

```python
import math
import jax, jax.numpy as jnp
from jax import lax
import numpy as np

D_MODEL = 1024
BATCH = 8
SEQ = 4096
DEPTH = 1

CHUNK = 64
Q_BLOCK = 128
N_MEM = 256
HEAD_DIM = 64
A_HEADS = 4
A_QK_DIM = HEAD_DIM
A_V_DIM = 2 * HEAD_DIM
B_HEADS = 8
B_DIM = HEAD_DIM
B_LEFT_CHUNKS = 8
B_BAND = B_LEFT_CHUNKS + 1
REL_CLIP = 128
C_HEADS = 4
C_DIM = 128
D_FF = 2816
ROPE_THETA = 10000.0
EPS = 1e-6
NEG = -1e30
N_BRANCH = 3

A_Q = A_HEADS * 2 * A_QK_DIM
A_V = A_HEADS * A_V_DIM
B_W = B_HEADS * B_DIM
C_W = C_HEADS * C_DIM
BRANCH_W = 512
IN_COLS = 2 * A_Q + A_V + 3 * B_W + C_W

kernel_name = "hybrid_streaming_diff_chunk_mem_layer"


def rmsnorm(x, g):
    xf = x.astype(jnp.float32)
    y = xf * lax.rsqrt(jnp.mean(xf * xf, axis=-1, keepdims=True) + EPS)
    return (y * g.astype(jnp.float32)).astype(x.dtype)


def rope(x):
    s, d = x.shape[1], x.shape[-1]
    half = d // 2
    inv = ROPE_THETA ** (-jnp.arange(half, dtype=jnp.float32) / half)
    ang = jnp.arange(s, dtype=jnp.float32)[:, None] * inv[None, :]
    shp = (1, s) + (1,) * (x.ndim - 3) + (half,)
    cos, sin = jnp.cos(ang).reshape(shp), jnp.sin(ang).reshape(shp)
    xf = x.astype(jnp.float32)
    x1, x2 = xf[..., :half], xf[..., half:]
    return jnp.concatenate([x1 * cos - x2 * sin, x2 * cos + x1 * sin], axis=-1).astype(x.dtype)


def swiglu(h, wg, wu, wd):
    return (jax.nn.silu(h @ wg) * (h @ wu)) @ wd


def diff_attention(q, k, v, lam, lambda_init, sub_gain):
    b, s, h = q.shape[0], q.shape[1], q.shape[2]
    scale = A_QK_DIM ** -0.5
    chunk_id = jnp.arange(s) // CHUNK
    outs = []
    for i in range(s // Q_BLOCK):
        q0, q1 = i * Q_BLOCK, (i + 1) * Q_BLOCK
        kb, vb = k[:, :q1], v[:, :q1]
        sc = jnp.einsum('bqhcd,bkhcd->bhcqk', q[:, q0:q1], kb).astype(jnp.float32) * scale
        mask = chunk_id[q0:q1, None] >= chunk_id[None, :q1]
        sc = jnp.where(mask[None, None, None], sc, NEG)
        p = jax.nn.softmax(sc, axis=-1)
        a = p[:, :, 0] - lam * p[:, :, 1]
        outs.append(jnp.einsum('bhqk,bkhd->bqhd', a.astype(v.dtype), vb))
    o = jnp.concatenate(outs, axis=1)
    o = rmsnorm(o, sub_gain) * (1.0 - lambda_init)
    return o.reshape(b, s, h * A_V_DIM)


def chunk_band_attention(q, k, v, rel_table):
    b, s, h, d = q.shape
    nc = s // CHUNK
    scale = d ** -0.5
    qc = q.reshape(b, nc, CHUNK, h, d)
    pad = ((0, 0), (B_LEFT_CHUNKS, 0), (0, 0), (0, 0), (0, 0))
    kp = jnp.pad(k.reshape(b, nc, CHUNK, h, d), pad)
    vp = jnp.pad(v.reshape(b, nc, CHUNK, h, d), pad)
    band_idx = jnp.arange(nc)[:, None] + jnp.arange(B_BAND)[None, :]
    kb = kp[:, band_idx].reshape(b, nc, B_BAND * CHUNK, h, d)
    vb = vp[:, band_idx].reshape(b, nc, B_BAND * CHUNK, h, d)
    sc = jnp.einsum('bcqhd,bckhd->bchqk', qc, kb).astype(jnp.float32) * scale
    qpos = jnp.arange(CHUNK) + B_LEFT_CHUNKS * CHUNK
    kpos = jnp.arange(B_BAND * CHUNK)
    rel = jnp.clip(qpos[:, None] - kpos[None, :], -REL_CLIP, REL_CLIP) + REL_CLIP
    sc = sc + rel_table.astype(jnp.float32)[:, rel][None, None]
    valid = jnp.repeat(band_idx >= B_LEFT_CHUNKS, CHUNK, axis=1)
    sc = jnp.where(valid[None, :, None, None, :], sc, NEG)
    p = jax.nn.softmax(sc, axis=-1)
    o = jnp.einsum('bchqk,bckhd->bcqhd', p.astype(v.dtype), vb)
    return o.reshape(b, s, h * d)


def memory_attention(q, mk, mv):
    b, s, h, d = q.shape
    sc = jnp.einsum('bshd,bmhd->bhsm', q, mk).astype(jnp.float32) * (d ** -0.5)
    p = jax.nn.softmax(sc, axis=-1)
    return jnp.einsum('bhsm,bmhd->bshd', p.astype(mv.dtype), mv).reshape(b, s, h * d)


def setup_inputs(seed: int = 0) -> dict:
    key = jax.random.key(seed)
    ks = iter(jax.random.split(key, 40))
    L, D, F = DEPTH, D_MODEL, D_FF

    def w(shape, fan_in):
        return jax.random.normal(next(ks), shape, jnp.float32) * (fan_in ** -0.5)

    def gain(shape):
        return 1.0 + 0.01 * jax.random.normal(next(ks), shape, jnp.float32)

    def small(shape, s):
        return s * jax.random.normal(next(ks), shape, jnp.float32)

    return {
        "x": jax.random.normal(next(ks), (BATCH, SEQ, D), jnp.float32),
        "mem": jax.random.normal(next(ks), (BATCH, N_MEM, D), jnp.float32),
        "ffn1_norm": gain((L, D)),
        "ffn1_wg": w((L, D, F), D),
        "ffn1_wu": w((L, D, F), D),
        "ffn1_wd": w((L, F, D), F),
        "mix_norm": gain((L, D)),
        "w_in": w((L, D, IN_COLS), D),
        "a_q_norm": gain((L, A_QK_DIM)),
        "a_k_norm": gain((L, A_QK_DIM)),
        "a_lambda": small((L, 4, A_QK_DIM), 0.1),
        "a_sub_norm": gain((L, A_V_DIM)),
        "b_q_norm": gain((L, B_DIM)),
        "b_k_norm": gain((L, B_DIM)),
        "b_rel_bias": small((L, B_HEADS, 2 * REL_CLIP + 1), 0.1),
        "mem_norm": gain((L, D)),
        "w_mem_kv": w((L, D, 2 * C_W), D),
        "c_q_norm": gain((L, C_DIM)),
        "c_k_norm": gain((L, C_DIM)),
        "w_gate": w((L, D, N_BRANCH * D), D),
        "b_gate": small((L, N_BRANCH * D), 0.01),
        "w_branch": w((L, N_BRANCH, BRANCH_W, D), BRANCH_W),
        "w_out": w((L, D, D), D),
        "ffn2_norm": gain((L, D)),
        "ffn2_wg": w((L, D, F), D),
        "ffn2_wu": w((L, D, F), D),
        "ffn2_wd": w((L, F, D), F),
        "final_norm": gain((L, D)),
    }


def reference(x, mem, ffn1_norm, ffn1_wg, ffn1_wu, ffn1_wd, mix_norm, w_in,
              a_q_norm, a_k_norm, a_lambda, a_sub_norm, b_q_norm, b_k_norm, b_rel_bias,
              mem_norm, w_mem_kv, c_q_norm, c_k_norm, w_gate, b_gate, w_branch, w_out,
              ffn2_norm, ffn2_wg, ffn2_wu, ffn2_wd, final_norm):
    b, s, d = x.shape
    m = mem.shape[1]
    split_at = list(np.cumsum([A_Q, A_Q, A_V, B_W, B_W, B_W]))
    for l in range(DEPTH):
        x = x + 0.5 * swiglu(rmsnorm(x, ffn1_norm[l]), ffn1_wg[l], ffn1_wu[l], ffn1_wd[l])

        h = rmsnorm(x, mix_norm[l])
        proj = h @ w_in[l]
        aq, ak, av, bq, bk, bv, cq = jnp.split(proj, [int(i) for i in split_at], axis=-1)

        lambda_init = 0.8 - 0.6 * math.exp(-0.3 * l)
        lp = a_lambda[l].astype(jnp.float32)
        lam = jnp.exp(jnp.sum(lp[0] * lp[1])) - jnp.exp(jnp.sum(lp[2] * lp[3])) + lambda_init
        aq = rope(rmsnorm(aq.reshape(b, s, A_HEADS, 2, A_QK_DIM), a_q_norm[l]))
        ak = rope(rmsnorm(ak.reshape(b, s, A_HEADS, 2, A_QK_DIM), a_k_norm[l]))
        av = av.reshape(b, s, A_HEADS, A_V_DIM)
        o_a = diff_attention(aq, ak, av, lam, lambda_init, a_sub_norm[l])

        bq = rmsnorm(bq.reshape(b, s, B_HEADS, B_DIM), b_q_norm[l])
        bk = rmsnorm(bk.reshape(b, s, B_HEADS, B_DIM), b_k_norm[l])
        bv = bv.reshape(b, s, B_HEADS, B_DIM)
        o_b = chunk_band_attention(bq, bk, bv, b_rel_bias[l])

        mkv = rmsnorm(mem, mem_norm[l]) @ w_mem_kv[l]
        mk, mv = jnp.split(mkv, 2, axis=-1)
        mk = rmsnorm(mk.reshape(b, m, C_HEADS, C_DIM), c_k_norm[l])
        mv = mv.reshape(b, m, C_HEADS, C_DIM)
        cq = rmsnorm(cq.reshape(b, s, C_HEADS, C_DIM), c_q_norm[l])
        o_c = memory_attention(cq, mk, mv)

        gates = jax.nn.sigmoid(h @ w_gate[l] + b_gate[l]).reshape(b, s, N_BRANCH, d)
        branches = jnp.stack([o_a, o_b, o_c], axis=2)
        y = jnp.einsum('bsnc,ncd,bsnd->bsd', branches, w_branch[l], gates)
        x = x + y @ w_out[l]

        x = x + 0.5 * swiglu(rmsnorm(x, ffn2_norm[l]), ffn2_wg[l], ffn2_wu[l], ffn2_wd[l])
        x = rmsnorm(x, final_norm[l])
    return x
```

```python
import math
from contextlib import ExitStack

import numpy as np
import concourse.bass as bass
import concourse.mybir as mybir
from concourse.bass_utils import run_bass_kernel_spmd

F32 = mybir.dt.float32
BF16 = mybir.dt.bfloat16
AF = mybir.ActivationFunctionType
ALU = mybir.AluOpType

D = 1024
SEQ = 4096
NMEM = 256
DFF = 2816
NFC = DFF // 128
T = 512
EPS = 1e-6
NSLOT = 4
SLOTW = 4096
GRAN = 256
SAME_ENGINE_SYNC = True
BQ = "pool"
SUMENG = "pool"
EW2 = "pool"
LAMBDA_INIT = 0.8 - 0.6 * math.exp(-0.3 * 0)

PV_FFN1, PV_MIX, PV_FFN2, PV_FIN, PV_MEMN = 0, 8, 16, 24, 32
PV_BG = 40
PV_AQ, PV_AK, PV_ASUB, PV_BQ, PV_BK, PV_CQ, PV_CK = 64, 65, 66, 67, 68, 69, 70
PV_LAM = 71
NPV = PV_LAM + 256
CM_INV1024, CM_BD64, CM_C128, CM_ONES, CM_IDENT, CM_ROT, CM_MASKA, CM_ONESL, CM_ONESR, CM_SEL0, CM_SEL1 = range(11)
NCM = 11
NEG = -30000.0


def _fm(w, c0, cw=128):
    k = w.shape[0]
    nk = k // 128
    return np.ascontiguousarray(
        w[:, c0:c0 + cw].reshape(nk, 128, cw).transpose(1, 0, 2).reshape(128, nk * cw))


def build_units():
    units = []
    units.append(("mk", 4096))
    units.append(("mv", 4096))
    for f in (1, 2):
        for i in range(11):
            units.append((f"ffn{f}_up{i}", 4096))
        for dc in range(8):
            units.append((f"ffn{f}_dn{dc}", NFC * 128))
        if f == 1:
            for nm in ("ak", "av", "aq", "bk", "bv", "bq", "cq"):
                units.append((f"in_{nm}", 4096))
            for dc in range(8):
                units.append((f"gate{dc}", 3072))
                if dc % 2 == 0:
                    units.append((f"br{dc // 2}", 3072))
            units.append(("wo0", 4096))
            units.append(("wo1", 4096))
    offs = {}
    o = 0
    for nm, w in units:
        offs[nm] = (o, w)
        o += w
    return units, offs, o


UNITS, UOFF, WALLW = build_units()


def build_wall(inp):
    parts = {}
    wkv = inp["w_mem_kv"][0]
    parts["mk"] = np.concatenate([_fm(wkv, h * 128) for h in range(4)], axis=1)
    parts["mv"] = _fm(wkv, 512, 512)
    for f in (1, 2):
        wg, wu, wd = inp[f"ffn{f}_wg"][0], inp[f"ffn{f}_wu"][0], inp[f"ffn{f}_wd"][0]
        for i in range(11):
            parts[f"ffn{f}_up{i}"] = np.concatenate(
                [_fm(wg, (2 * i) * 128), _fm(wu, (2 * i) * 128),
                 _fm(wg, (2 * i + 1) * 128), _fm(wu, (2 * i + 1) * 128)], axis=1)
        for dc in range(8):
            parts[f"ffn{f}_dn{dc}"] = _fm(wd, dc * 128)
    win = inp["w_in"][0]
    parts["in_aq"] = np.concatenate([_fm(win, 0 + h * 128) for h in range(4)], axis=1)
    parts["in_ak"] = np.concatenate([_fm(win, 512 + h * 128) for h in range(4)], axis=1)
    parts["in_av"] = _fm(win, 1024, 512)
    parts["in_bq"] = np.concatenate([_fm(win, 1536 + h * 128) for h in range(4)], axis=1)
    parts["in_bk"] = np.concatenate([_fm(win, 2048 + h * 128) for h in range(4)], axis=1)
    parts["in_bv"] = _fm(win, 2560, 512)
    parts["in_cq"] = np.concatenate([_fm(win, 3072 + h * 128) for h in range(4)], axis=1)
    wgt = inp["w_gate"][0]
    wbr = inp["w_branch"][0]
    for dc in range(8):
        parts[f"gate{dc}"] = np.concatenate(
            [_fm(wgt, n * 1024 + dc * 128) for n in range(3)], axis=1)
    for pr in range(4):
        parts[f"br{pr}"] = np.concatenate(
            [_fm(wbr[n], dc * 128) for dc in (2 * pr, 2 * pr + 1) for n in range(3)], axis=1)
    wo = inp["w_out"][0]
    parts["wo0"] = np.concatenate([_fm(wo, dc * 128) for dc in range(4)], axis=1)
    parts["wo1"] = np.concatenate([_fm(wo, dc * 128) for dc in range(4, 8)], axis=1)
    wall = np.empty((128, WALLW), np.float32)
    for nm, w in UNITS:
        o, _ = UOFF[nm]
        assert parts[nm].shape == (128, w), (nm, parts[nm].shape, w)
        wall[:, o:o + w] = parts[nm]
    return wall


def build_pvec(inp):
    pv = np.zeros((128, NPV), np.float32)

    def colmajor(v):
        return v.reshape(8, 128).T

    pv[:, PV_FFN1:PV_FFN1 + 8] = colmajor(inp["ffn1_norm"][0])
    pv[:, PV_MIX:PV_MIX + 8] = colmajor(inp["mix_norm"][0])
    pv[:, PV_FFN2:PV_FFN2 + 8] = colmajor(inp["ffn2_norm"][0])
    pv[:, PV_FIN:PV_FIN + 8] = colmajor(inp["final_norm"][0])
    pv[:, PV_MEMN:PV_MEMN + 8] = colmajor(inp["mem_norm"][0])
    bg = inp["b_gate"][0]
    for dc in range(8):
        for n in range(3):
            pv[:, PV_BG + dc * 3 + n] = bg[n * 1024 + dc * 128: n * 1024 + dc * 128 + 128]
    p = np.arange(128)
    pv[:, PV_AQ] = inp["a_q_norm"][0][p % 64]
    pv[:, PV_AK] = inp["a_k_norm"][0][p % 64]
    pv[:, PV_ASUB] = inp["a_sub_norm"][0]
    pv[:, PV_BQ] = inp["b_q_norm"][0][p % 64]
    pv[:, PV_BK] = inp["b_k_norm"][0][p % 64]
    pv[:, PV_CQ] = inp["c_q_norm"][0]
    pv[:, PV_CK] = inp["c_k_norm"][0]
    pv[:, PV_LAM:PV_LAM + 256] = inp["a_lambda"][0].reshape(1, 256)
    return pv


def build_consts(S):
    cm = np.zeros((128, NCM, 128), np.float32)
    cm[:, CM_INV1024] = 1.0 / 1024
    for b in range(2):
        cm[b * 64:(b + 1) * 64, CM_BD64, b * 64:(b + 1) * 64] = 1.0 / 64
    cm[:, CM_C128] = 1.0 / 128
    cm[:, CM_ONES] = 1.0
    cm[:, CM_IDENT] = np.eye(128, dtype=np.float32)
    for m in range(128):
        if (m % 64) < 32:
            cm[m + 32, CM_ROT, m] = -1.0
        else:
            cm[m - 32, CM_ROT, m] = 1.0
    k = np.arange(128)[:, None]
    q = np.arange(128)[None, :]
    cm[:, CM_MASKA] = np.where((k // 64) > (q // 64), NEG, 0.0)
    cm[:, CM_ONESL, 0:64] = 1.0
    cm[:, CM_ONESR, 64:128] = 1.0
    cm[0:64, CM_SEL0, :] = 1.0 / 64
    cm[64:128, CM_SEL1, :] = 1.0 / 64
    half = 32
    inv = (10000.0 ** (-np.arange(half, dtype=np.float32) / half)).astype(np.float32)
    pos = np.arange(S, dtype=np.float32)
    ang = (pos[:, None] * inv[None, :]).astype(np.float32)
    i_of_p = (np.arange(128) % 64) % 32
    cosT = np.ascontiguousarray(np.cos(ang).astype(np.float32)[:, i_of_p].T)
    sinT = np.ascontiguousarray(np.sin(ang).astype(np.float32)[:, i_of_p].T)
    return cm.reshape(128, NCM * 128), cosT, sinT


def build_btab(rel):
    k = np.arange(128)[:, None]
    c = np.arange(640)[None, :]
    idx = np.clip(c - k, -128, 128) + 128
    bt = rel[:, idx]
    return np.ascontiguousarray(bt.transpose(1, 0, 2)).astype(np.float32)


_DSZ = {F32: 4, BF16: 2}


class _Op:
    __slots__ = ("eng", "fn", "waits", "epoch", "pos", "dma", "flag")


class Sched:
    ENGS = ("pe", "act", "dve", "pool", "sp")

    def __init__(self):
        self.ops = {e: [] for e in self.ENGS}
        self.known = {e: {} for e in self.ENGS}
        self.selfk = {e: 0 for e in self.ENGS}
        self.gran = {}
        self.clock = {}
        self.dma_cnt = {}
        self.epoch = 0
        self.flagged = set()

    @staticmethod
    def keys(ap):
        if isinstance(ap, tuple):
            return [ap]
        t = ap.tensor
        name = t.name
        if name.startswith("psb"):
            return [(name,)]
        esz = _DSZ[ap.dtype]
        shp = tuple(t.shape)
        pstride = 1
        for s in shp[1:]:
            pstride *= s
        off = int(ap.offset) % pstride
        dims = ap.ap
        hi = off
        for st, cnt in dims[1:]:
            hi += (cnt - 1) * st
        lo_b = off * esz
        hi_b = (hi + 1) * esz
        return [(name, g) for g in range(lo_b // GRAN, (hi_b - 1) // GRAN + 1)]

    def add(self, eng, fn, reads=(), writes=(), dma=None):
        deps = []
        rk = []
        for r in reads:
            rk.extend(self.keys(r))
        wk = []
        for w in writes:
            wk.extend(self.keys(w))
        for g in rk:
            st = self.gran.get(g)
            if st is not None and st[0] is not None:
                deps.append(st[0])
        for g in wk:
            st = self.gran.get(g)
            if st is not None:
                if st[0] is not None:
                    deps.append(st[0])
                deps.extend(st[1])
        op = _Op()
        op.eng = eng
        op.fn = fn
        op.epoch = self.epoch
        op.dma = None
        op.flag = False
        known = self.known[eng]
        waits = {}
        for ev in sorted(set(deps), key=lambda z: (z[0], str(z[1]), z[2]), reverse=True):
            kind, key, val = ev
            if kind == "c" and key == eng:
                if eng == "pe" or eng == "sp" or not SAME_ENGINE_SYNC:
                    continue
                if self.selfk[eng] >= val:
                    continue
                self.selfk[eng] = val
                if waits.get((kind, key), 0) < val:
                    waits[(kind, key)] = val
                continue
            kk = key if kind == "c" else ("d", key)
            if known.get(kk, 0) >= val:
                continue
            if waits.get((kind, key), 0) < val:
                waits[(kind, key)] = val
        for (kind, key), val in waits.items():
            ck = self.clock[(kind, key, val)]
            for a, b in ck.items():
                if known.get(a, 0) < b:
                    known[a] = b
            if kind == "c":
                self.flagged.add((key, val))
        op.waits = list(waits.items())
        self.ops[eng].append(op)
        op.pos = len(self.ops[eng])
        if dma is None:
            ev = ("c", eng, op.pos)
            ck = dict(known)
            ck[eng] = op.pos
            known[eng] = op.pos
        else:
            cnt = self.dma_cnt.get(dma, 0) + 16
            self.dma_cnt[dma] = cnt
            op.dma = (dma, cnt)
            ev = ("d", dma, cnt)
            ck = dict(known)
            ck[("d", dma)] = cnt
        self.clock[ev] = ck
        for g in rk:
            st = self.gran.setdefault(g, [None, []])
            lst = st[1]
            if ev[0] == "c":
                lst[:] = [e for e in lst if not (e[0] == "c" and e[1] == ev[1])]
            lst.append(ev)
        for g in wk:
            self.gran[g] = [ev, []]
        return op

    def emit(self, nc, es):
        groups = {}
        for e in self.ENGS:
            for op in self.ops[e]:
                if op.dma is None and (e, op.pos) in self.flagged:
                    op.flag = True
                    groups.setdefault((e, op.epoch), []).append(op.pos)
        sem = {}
        rank = {}
        for (e, ep), lst in groups.items():
            sem[(e, ep)] = es.enter_context(nc.semaphore(f"s_{e}_{ep}"))
            for i, p in enumerate(lst):
                rank[(e, p)] = (sem[(e, ep)], i + 1)
        dsem = {}
        for nm in self.dma_cnt:
            dsem[nm] = es.enter_context(nc.semaphore(f"d_{nm}"))
        block = es.enter_context(nc.Block())

        def run(engobj, e):
            for op in self.ops[e]:
                for (kind, key), val in op.waits:
                    if kind == "c":
                        s, v = rank[(key, val)]
                    else:
                        s, v = dsem[key], val
                    engobj.wait_ge(s, v)
                if op.fn is None:
                    continue
                inst = op.fn(engobj)
                if op.dma is not None:
                    inst.then_inc(dsem[op.dma[0]], 16)
                elif op.flag:
                    s, _ = rank[(e, op.pos)]
                    inst.then_inc(s, 1)

        @block.tensor
        def _(eng):
            run(eng, "pe")

        @block.scalar
        def _(eng):
            run(eng, "act")

        @block.vector
        def _(eng):
            run(eng, "dve")

        @block.gpsimd
        def _(eng):
            run(eng, "pool")

        @block.sync
        def _(eng):
            run(eng, "sp")


def build_program(S=SEQ, debug=False):
    NT = S // T
    NKB = S // 128
    nc = bass.Bass("TRN2", target_bir_lowering=False)
    xT = nc.dram_tensor("xT", [D, S], F32, kind="ExternalInput").ap()
    memT = nc.dram_tensor("memT", [D, NMEM], F32, kind="ExternalInput").ap()
    wall = nc.dram_tensor("wall", [128, WALLW], F32, kind="ExternalInput").ap()
    pvec_d = nc.dram_tensor("pvec", [128, NPV], F32, kind="ExternalInput").ap()
    cmat_d = nc.dram_tensor("cmat", [128, NCM * 128], F32, kind="ExternalInput").ap()
    cos_d = nc.dram_tensor("cosT", [128, S], F32, kind="ExternalInput").ap()
    sin_d = nc.dram_tensor("sinT", [128, S], F32, kind="ExternalInput").ap()
    btab_d = nc.dram_tensor("btab", [128, 8, 640], F32, kind="ExternalInput").ap()
    outT = nc.dram_tensor("outT", [D, S], F32, kind="ExternalOutput").ap()
    scr = nc.dram_tensor("scr", [128, WALLW], BF16, kind="Internal").ap()

    sc = Sched()
    es = ExitStack()

    def sb(name, shape, dt):
        return es.enter_context(nc.sbuf_tensor(name, shape, dt))

    akT = sb("akT", [128, 4, S], BF16)
    avc = sb("avc", [128, NKB, 512], BF16)
    bkT = sb("bkT", [128, 4, 1024], BF16)
    bvc = sb("bvc", [128, 8, 512], BF16)
    x_sb = sb("x_sb", [128, 8, T], F32)
    hT = sb("hT", [128, 8, T], BF16)
    U = sb("U", [128, 24, T], BF16)
    wring = sb("wring", [128, NSLOT, SLOTW], BF16)
    cos_sb = sb("cos_sb", [128, T], F32)
    sin_sb = sb("sin_sb", [128, T], F32)
    BT = sb("BT", [128, 8, 640], BF16)
    mkT = sb("mkT", [128, 4, NMEM], BF16)
    mvc = sb("mvc", [128, 2, 512], BF16)
    cm = sb("cm", [128, NCM, 128], BF16)
    pv = sb("pv", [128, NPV], F32)
    sm = sb("sm", [128, 16], F32)
    sqh = sb("sqh", [128, 2, T], BF16)
    ftmp = sb("ftmp", [128, 6, T], F32)
    etmp = sb("etmp", [128, 6, T], BF16)
    qn = sb("qn", [128, 2, T], BF16)
    rnext = sb("rnext", [128, T], F32)
    ps = [es.enter_context(nc.psum_tensor(f"psb{i}", [128, 512], F32)) for i in range(8)]

    def hid(fc):
        return U[:, fc, :]

    SM_NEGLAM, SM_GSUB, SM_EPS, SM_HB0 = 0, 1, 2, 3

    smb = sb("smb", [128, 24], F32)

    cnt = {"ps": 0, "f": 0, "e": 0, "q": 0, "s": 0}
    ps_pool = {"banks": list(range(8))}

    def ps_next():
        b = ps_pool["banks"][cnt["ps"] % len(ps_pool["banks"])]
        cnt["ps"] += 1
        return ps[b]

    def f_next():
        i = cnt["f"] % 6
        cnt["f"] += 1
        return ftmp[:, i, :]

    def e_next():
        i = cnt["e"] % 6
        cnt["e"] += 1
        return etmp[:, i, :]

    def q_next():
        i = cnt["q"] % 2
        cnt["q"] += 1
        return qn[:, i, :]

    def s_next():
        i = cnt["s"] % 2
        cnt["s"] += 1
        return sqh[:, i, :]

    def C(i, m=128):
        return cm[:, i, 0:m]

    def mm(out, lhsT, rhs, start=True, stop=True, **kw):
        sc.add("pe", lambda e: e.matmul(out, lhsT, rhs, start=start, stop=stop, **kw),
               reads=[lhsT, rhs], writes=[out])

    def act(out, in_, func, bias=None, scale=1.0):
        rd = [in_]
        if bias is not None and not isinstance(bias, float):
            rd.append(bias)
        if bias is None:
            sc.add("act", lambda e: e.activation(out=out, in_=in_, func=func, scale=scale),
                   reads=rd, writes=[out])
        else:
            sc.add("act", lambda e: e.activation(out=out, in_=in_, func=func, bias=bias, scale=scale),
                   reads=rd, writes=[out])

    def tt(out, in0, in1, op, eng="dve"):
        sc.add(eng, lambda e: e.tensor_tensor(out=out, in0=in0, in1=in1, op=op),
               reads=[in0, in1], writes=[out])

    def stt(out, in0, scalar, in1, op0, op1, eng="dve"):
        rd = [in0, in1]
        if not isinstance(scalar, float):
            rd.append(scalar)
        sc.add(eng, lambda e: e.scalar_tensor_tensor(out=out, in0=in0, scalar=scalar, in1=in1,
                                                     op0=op0, op1=op1),
               reads=rd, writes=[out])

    def ts(out, in0, s1, op0, eng="dve"):
        rd = [in0]
        if not isinstance(s1, float):
            rd.append(s1)
        sc.add(eng, lambda e: e.tensor_scalar(out=out, in0=in0, scalar1=s1, scalar2=None, op0=op0),
               reads=rd, writes=[out])

    def recip(out, in_):
        act(out, in_, AF.Ln)
        act(out, out, AF.Exp, scale=-1.0)

    def memset(ap, val, eng="dve"):
        sc.add(eng, lambda e: e.memset(ap, val), reads=[], writes=[ap])

    def dma(eng, out, in_, sem, reads=(), writes=()):
        sc.add(eng, lambda e: e.dma_start(out=out, in_=in_), reads=list(reads), writes=list(writes), dma=sem)

    stream = []
    stream += ["mk", "mv"]
    per_tile = []
    per_tile += [f"ffn1_up{i}" for i in range(11)] + [f"ffn1_dn{dc}" for dc in range(8)]
    per_tile += ["in_ak", "in_av", "in_aq", "in_bk", "in_bv", "in_bq", "in_cq"]
    for dc in range(8):
        per_tile.append(f"gate{dc}")
        if dc % 2 == 0:
            per_tile.append(f"br{dc // 2}")
    per_tile += ["wo0", "wo1"]
    per_tile += [f"ffn2_up{i}" for i in range(11)] + [f"ffn2_dn{dc}" for dc in range(8)]
    for t in range(NT):
        stream += per_tile
    st = {"next_load": 0, "next_use": 0}

    LA = 6
    UIDX = {nm: k for k, (nm, w) in enumerate(UNITS)}

    def cast_unit(k, rd):
        o, w = UOFF[UNITS[k][0]]
        dma("pool", scr[:, o:o + w], wall[:, o:o + w], f"pre{k % LA}", reads=rd, writes=[("scr", k)])

    def load_unit(idx):
        nm = stream[idx]
        o, w = UOFF[nm]
        slot = idx % NSLOT
        wr = [wring[:, slot, 0:w]]
        if idx < len(UNITS):
            wr.append(("wl", idx))
        dma("sp", wring[:, slot, 0:w], scr[:, o:o + w], f"w{slot}",
            reads=[("scr", UIDX[nm])], writes=wr)
        if idx + LA < len(UNITS):
            cast_unit(idx + LA, [("wl", idx)])

    def use_unit(nm):
        i = st["next_use"]
        assert stream[i] == nm, (stream[i], nm)
        st["next_use"] += 1
        while st["next_load"] < min(len(stream), i + NSLOT - 1):
            load_unit(st["next_load"])
            st["next_load"] += 1
        return i % NSLOT

    def Wv(slot, off, n):
        return wring[:, slot, off:off + n]

    dma("pool", cm[:, :, :], cmat_d.rearrange("p (a b) -> p a b", a=NCM), "const", writes=[cm[:, :, :]])
    dma("sp", pv[:, :], pvec_d[:, :], "pvl", writes=[pv[:, :]])
    for k in range(LA):
        cast_unit(k, [])
    memset(sm[:, SM_EPS:SM_EPS + 1], EPS)
    ts(sm[:, SM_GSUB:SM_GSUB + 1], pv[:, PV_ASUB:PV_ASUB + 1], 1.0 - LAMBDA_INIT, ALU.mult)
    ts(smb[:, :], pv[:, PV_BG:PV_BG + 24], -1.0, ALU.mult)
    memset(sm[:, 8:9], 1.0)
    lamt = ftmp[:, 0, 0:128]
    tt(lamt[:, 0:64], pv[:, PV_LAM:PV_LAM + 64], pv[:, PV_LAM + 64:PV_LAM + 128], ALU.mult)
    tt(lamt[:, 64:128], pv[:, PV_LAM + 128:PV_LAM + 192], pv[:, PV_LAM + 192:PV_LAM + 256], ALU.mult)
    sc.add("dve", lambda e: e.reduce_sum(out=sm[:, 4:5], in_=lamt[:, 0:64], axis=mybir.AxisListType.X),
           reads=[lamt[:, 0:64]], writes=[sm[:, 4:5]])
    sc.add("dve", lambda e: e.reduce_sum(out=sm[:, 5:6], in_=lamt[:, 64:128], axis=mybir.AxisListType.X),
           reads=[lamt[:, 64:128]], writes=[sm[:, 5:6]])
    act(sm[:, 6:8], sm[:, 4:6], AF.Exp)
    stt(sm[:, SM_NEGLAM:SM_NEGLAM + 1], sm[:, 7:8], -LAMBDA_INIT, sm[:, 6:7], ALU.add, ALU.subtract)
    eps_ap = sm[:, SM_EPS:SM_EPS + 1]
    one_ap = sm[:, 8:9]

    xflat = x_sb[:, :, :].rearrange("p a b -> p (a b)")
    for hh in range(2):
        stg = xflat[:, 0:2560].rearrange("p (h c) -> p h c", h=4)
        dma("sp", stg, btab_d[:, hh * 4:(hh + 1) * 4, :], "bt", writes=[stg])
        sc.add("act", lambda e, stg=stg, hh=hh: e.mul(out=BT[:, hh * 4:(hh + 1) * 4, :], in_=stg, mul=8.0),
               reads=[stg], writes=[BT[:, hh * 4:(hh + 1) * 4, :]])
    memset(BT[64:128, :, 0:64], NEG)
    memset(BT[0:64, :, 4 * 128 + 64:4 * 128 + 128], NEG)
    act(BT[:, :, :], BT[:, :, :], AF.Exp, scale=0.125)

    def rmsnorm_x(src, n, gcol, dst):
        sq = U[:, 0:8, 0:n]
        act(U[:, 0:4, 0:n], src[:, 0:4, 0:n], AF.Square)
        act(U[:, 4:8, 0:n], src[:, 4:8, 0:n], AF.Square)
        p = ps_next()
        for kc in range(8):
            mm(p[:, 0:n], C(CM_INV1024), U[:, kc, 0:n], start=(kc == 0), stop=(kc == 7))
        r = f_next()
        act(r[:, 0:n], p[:, 0:n], AF.Ln, bias=eps_ap)
        act(r[:, 0:n], r[:, 0:n], AF.Exp, scale=-0.5)
        for kc in range(8):
            stt(dst[:, kc, 0:n], src[:, kc, 0:n], pv[:, gcol + kc:gcol + kc + 1], r[:, 0:n],
                ALU.mult, ALU.mult)

    xsrc = xT.rearrange("(k p) s -> p k s", p=128)
    NB = 7

    def fused_sq(dc):
        sqb = U[:, 22 + (dc % 2), :]
        act(sqb, x_sb[:, dc, :], AF.Square)
        mm(ps[NB][:, :], C(CM_INV1024), sqb, start=(dc == 0), stop=(dc == 7), skip_group_check=True)

    def finish_norm(gcol, dst_of_kc, r=None):
        if r is None:
            r = f_next()
        act(r, ps[NB][:, :], AF.Ln, bias=eps_ap)
        act(r, r, AF.Exp, scale=-0.5)
        for kc in range(8):
            stt(dst_of_kc(kc), x_sb[:, kc, :], pv[:, gcol + kc:gcol + kc + 1], r, ALU.mult, ALU.mult)

    Uf = U[:, 8:24, :].bitcast(F32).rearrange("p (k a) c -> p k (a c)", a=2)

    NB2 = 6

    def xstage(kc):
        if kc < 6:
            return ftmp[:, kc, :]
        return cos_sb[:, :] if kc == 6 else sin_sb[:, :]

    def sqstage(kc):
        return etmp[:, kc, :] if kc < 6 else qn[:, kc - 6, :]

    def prefetch_x(tn):
        dma(BQ, ftmp[:, 0:6, :], xsrc[:, 0:6, tn * T:(tn + 1) * T], "xs0", writes=[ftmp[:, 0:6, :]])
        dma(BQ, cos_sb[:, :], xsrc[:, 6, tn * T:(tn + 1) * T], "xs1", writes=[cos_sb[:, :]])
        dma(BQ, sin_sb[:, :], xsrc[:, 7, tn * T:(tn + 1) * T], "xs2", writes=[sin_sb[:, :]])

    def next_norm_part1():
        for kc in range(8):
            if kc % 2 == 0:
                act(sqstage(kc), xstage(kc), AF.Square)
            else:
                tt(sqstage(kc), xstage(kc), xstage(kc), ALU.mult)
            mm(ps[NB2][:, :], C(CM_INV1024), sqstage(kc), start=(kc == 0), stop=(kc == 7), skip_group_check=True)
        act(rnext[:, :], ps[NB2][:, :], AF.Ln, bias=eps_ap)
        act(rnext[:, :], rnext[:, :], AF.Exp, scale=-0.5)

    def next_norm_part2(k0, k1):
        for kc in range(k0, k1):
            stt(hT[:, kc, :], xstage(kc), pv[:, PV_FFN1 + kc:PV_FFN1 + kc + 1], rnext[:, :], ALU.mult, ALU.mult)


    def head_rstd(pq, n, cmi):
        s = s_next()
        act(s[:, 0:n], pq[:, 0:n], AF.Square)
        p = ps_next()
        mm(p[:, 0:n], C(cmi), s[:, 0:n])
        r = f_next()
        act(r[:, 0:n], p[:, 0:n], AF.Ln, bias=eps_ap)
        act(r[:, 0:n], r[:, 0:n], AF.Exp, scale=-0.5)
        return r

    def proj_fm(slot, ci, n):
        p = ps_next()
        for kc in range(8):
            mm(p[:, 0:n], Wv(slot, (ci * 8 + kc) * 128, 128), hT[:, kc, 0:n],
               start=(kc == 0), stop=(kc == 7))
        return p

    def normed(pq, n, cmi, gcol, dst):
        r = head_rstd(pq, n, cmi)
        stt(dst, pq[:, 0:n], pv[:, gcol:gcol + 1], r[:, 0:n], ALU.mult, ALU.mult)

    def roped(pq, gcol, dst):
        r = head_rstd(pq, T, CM_BD64)
        q = q_next()
        stt(q, pq[:, :], pv[:, gcol:gcol + 1], r, ALU.mult, ALU.mult)
        pr = ps_next()
        mm(pr[:, :], C(CM_ROT), q)
        t1 = f_next()
        tt(t1, q, cos_sb[:, :], ALU.mult)
        t2 = f_next()
        tt(t2, pr[:, :], sin_sb[:, :], ALU.mult)
        tt(dst, t1, t2, ALU.add)

    dma("sp", x_sb[:, :, 0:NMEM], memT.rearrange("(k p) s -> p k s", p=128), "meml",
        writes=[x_sb[:, :, 0:NMEM]])
    rmsnorm_x(x_sb, NMEM, PV_MEMN, hT)
    slot = use_unit("mk")
    for h in range(4):
        pq = proj_fm(slot, h, NMEM)
        normed(pq, NMEM, CM_C128, PV_CK, mkT[:, h, :])
    slot = use_unit("mv")
    for mb in range(2):
        p = ps_next()
        for kc in range(8):
            mm(p[:, :], hT[:, kc, mb * 128:(mb + 1) * 128], Wv(slot, kc * 512, 512),
               start=(kc == 0), stop=(kc == 7))
        sc.add("act", lambda e, p=p, mb=mb: e.copy(out=mvc[:, mb, :], in_=p[:, :]),
               reads=[p[:, :]], writes=[mvc[:, mb, :]])

    def ffn(f, prefetch_tile=None):
        ps_pool["banks"] = [0, 1, 2, 3, 4, 5, 6] if prefetch_tile is None else [0, 1, 2, 3, 4, 5]
        for i in range(11):
            slot = use_unit(f"ffn{f}_up{i}")
            for j in range(2):
                fc = 2 * i + j
                pg = ps_next()
                for kc in range(8):
                    mm(pg[:, :], Wv(slot, ((2 * j) * 8 + kc) * 128, 128), hT[:, kc, :],
                       start=(kc == 0), stop=(kc == 7))
                pu = ps_next()
                for kc in range(8):
                    mm(pu[:, :], Wv(slot, ((2 * j + 1) * 8 + kc) * 128, 128), hT[:, kc, :],
                       start=(kc == 0), stop=(kc == 7))
                sg = f_next()
                act(sg, pg[:, :], AF.Exp, scale=-1.0)
                act(sg, sg, AF.Ln, bias=one_ap)
                act(sg, sg, AF.Exp, scale=-1.0)
                tt(sg, sg, pg[:, :], ALU.mult)
                tt(hid(fc), sg, pu[:, :], ALU.mult)
        for dc in range(8):
            slot = use_unit(f"ffn{f}_dn{dc}")
            if prefetch_tile is not None and dc == 1:
                prefetch_x(prefetch_tile)
            pd = ps_next()
            for fc in range(NFC):
                mm(pd[:, :], Wv(slot, fc * 128, 128), hid(fc), start=(fc == 0), stop=(fc == NFC - 1))
            stt(x_sb[:, dc, :], pd[:, :], 0.5, x_sb[:, dc, :], ALU.mult, ALU.add)
            if dc >= 1:
                fused_sq(dc - 1)
            if prefetch_tile is not None:
                if dc == 4:
                    next_norm_part1()
                elif dc == 5:
                    next_norm_part2(0, 4)
                elif dc == 6:
                    next_norm_part2(4, 8)
        fused_sq(7)

    aqT = lambda h: U[:, 0 + h, :]
    bqT = lambda c: U[:, 4 + c, :]
    cqT = lambda h: U[:, 8 + h, :]
    oT = lambda n, k: U[:, 12 + 4 * n + k, :]
    yT = lambda k: U[:, k, :]

    tails = []

    def tick():
        for g in list(tails):
            try:
                next(g)
            except StopIteration:
                tails.remove(g)

    def drain():
        while tails:
            tick()

    def load_x(tn):
        for hf in range(2):
            dma("sp", x_sb[:, 4 * hf:4 * hf + 4, :], xsrc[:, 4 * hf:4 * hf + 4, tn * T:(tn + 1) * T], f"xl{hf}",
                writes=[x_sb[:, 4 * hf:4 * hf + 4, :]])

    for t in range(NT):
        sc.epoch = t + 1
        c0t = t * T
        ps_pool["banks"] = list(range(8))
        cnt["f"] = 0
        if t == 0:
            load_x(0)
        rq = "sp" if t == 0 else BQ
        dma(rq, cos_sb[:, :], cos_d[:, c0t:c0t + T], "ropec", writes=[cos_sb[:, :]])
        dma(rq, sin_sb[:, :], sin_d[:, c0t:c0t + T], "ropes", writes=[sin_sb[:, :]])

        ps_pool["banks"] = [0, 1, 2, 3, 4, 5, 6]
        if t == 0:
            for kc in range(8):
                if kc % 2 == 0:
                    act(U[:, kc, :], x_sb[:, kc, :], AF.Square)
                else:
                    tt(U[:, kc, :], x_sb[:, kc, :], x_sb[:, kc, :], ALU.mult)
                mm(ps[NB][:, :], C(CM_INV1024), U[:, kc, :], start=(kc == 0), stop=(kc == 7), skip_group_check=True)
            finish_norm(PV_FFN1, lambda kc: hT[:, kc, :])
        ffn(1)

        finish_norm(PV_MIX, lambda kc: hT[:, kc, :])
        ps_pool["banks"] = list(range(8))

        def job_fm(slot, ci, cmi, gcol, dst, rope):
            def gen():
                pq = proj_fm(slot, ci, T)
                sq_ = s_next()
                act(sq_, pq[:, :], AF.Square)
                yield
                p = ps_next()
                mm(p[:, :], C(cmi), sq_)
                r = f_next()
                act(r, p[:, :], AF.Ln, bias=eps_ap)
                act(r, r, AF.Exp, scale=-0.5)
                if not rope:
                    stt(dst, pq[:, :], pv[:, gcol:gcol + 1], r, ALU.mult, ALU.mult)
                    return
                q = q_next()
                stt(q, pq[:, :], pv[:, gcol:gcol + 1], r, ALU.mult, ALU.mult)
                yield
                pr = ps_next()
                mm(pr[:, :], C(CM_ROT), q)
                t1 = f_next()
                tt(t1, q, cos_sb[:, :], ALU.mult, eng=EW2)
                t2 = f_next()
                tt(t2, pr[:, :], sin_sb[:, :], ALU.mult)
                tt(dst, t1, t2, ALU.add)
            return gen

        def job_tm(slot, tb, dst):
            def gen():
                p = ps_next()
                for kc in range(8):
                    mm(p[:, :], hT[:, kc, tb * 128:(tb + 1) * 128], Wv(slot, kc * 512, 512),
                       start=(kc == 0), stop=(kc == 7))
                sc.add("act", lambda e: e.copy(out=dst, in_=p[:, :]), reads=[p[:, :]], writes=[dst])
                return
                yield
            return gen

        def run_jobs(jobs):
            active = []
            for jf in list(jobs) + [None, None, None]:
                prev = active
                active = []
                if jf is not None:
                    g = jf()
                    try:
                        next(g)
                        active.append(g)
                    except StopIteration:
                        pass
                for g in prev:
                    try:
                        next(g)
                        active.append(g)
                    except StopIteration:
                        pass

        jobs = []

        def lazy_unit(nm, mk):
            box = {}

            def wrap(i):
                def jf():
                    if "slot" not in box:
                        box["slot"] = use_unit(nm)
                    return mk(box["slot"], i)()
                return jf
            return wrap

        w = lazy_unit("in_ak", lambda sl, h: job_fm(sl, h, CM_BD64, PV_AK, akT[:, h, c0t:c0t + T], True))
        jobs += [w(h) for h in range(4)]
        w = lazy_unit("in_av", lambda sl, tb: job_tm(sl, tb, avc[:, t * 4 + tb, :]))
        jobs += [w(tb) for tb in range(4)]
        w = lazy_unit("in_aq", lambda sl, h: job_fm(sl, h, CM_BD64, PV_AQ, aqT(h), True))
        jobs += [w(h) for h in range(4)]
        w = lazy_unit("in_bk", lambda sl, c: job_fm(sl, c, CM_BD64, PV_BK, bkT[:, c, (t % 2) * T:(t % 2) * T + T], False))
        jobs += [w(c) for c in range(4)]
        w = lazy_unit("in_bv", lambda sl, tb: job_tm(sl, tb, bvc[:, (t * 4 + tb) % 8, :]))
        jobs += [w(tb) for tb in range(4)]
        w = lazy_unit("in_bq", lambda sl, c: job_fm(sl, c, CM_BD64, PV_BQ, bqT(c), False))
        jobs += [w(c) for c in range(4)]
        w = lazy_unit("in_cq", lambda sl, h: job_fm(sl, h, CM_C128, PV_CQ, cqT(h), False))
        jobs += [w(h) for h in range(4)]
        run_jobs(jobs)

        nkb = 4 * t + 4
        ps_pool["banks"] = [3, 4, 5, 6, 7]
        o0, o1, s01 = ps[0], ps[1], ps[2]

        def start_a_tail(h):
                def a_tail(h=h):
                    fa = f_next()
                    fb = f_next()
                    r = f_next()
                    sc.add("act", lambda e: e.copy(out=fa, in_=ps[0][:, :]), reads=[ps[0][:, :]], writes=[fa])
                    sc.add("dve", lambda e: e.tensor_copy(out=fb, in_=ps[1][:, :]), reads=[ps[1][:, :]], writes=[fb])
                    act(r, ps[2][:, :], AF.Ln)
                    yield
                    act(r, r, AF.Exp, scale=-1.0)
                    yield
                    rhi = s_next()
                    rlo = s_next()
                    sc.add("dve", lambda e: e.tensor_copy(out=rhi, in_=r), reads=[r], writes=[rhi])
                    tt(rlo, r, rhi, ALU.subtract)
                    yield
                    pbs_ = []
                    for comp in range(2):
                        pbc = ps_next()
                        sel = C(CM_SEL0 if comp == 0 else CM_SEL1)
                        mm(pbc[:, :], sel, rhi, start=True, stop=False)
                        mm(pbc[:, :], sel, rlo, start=False, stop=True)
                        pbs_.append(pbc)
                    yield
                    tt(fa, fa, pbs_[0][:, :], ALU.mult)
                    tt(fb, fb, pbs_[1][:, :], ALU.mult)
                    yield
                    stt(fa, fb, sm[:, SM_NEGLAM:SM_NEGLAM + 1], fa, ALU.mult, ALU.add)
                    yield
                    sq_ = s_next()
                    act(sq_, fa, AF.Square)
                    yield
                    p = ps_next()
                    mm(p[:, :], C(CM_C128), sq_)
                    yield
                    r2 = r
                    act(r2, p[:, :], AF.Ln, bias=eps_ap)
                    yield
                    act(r2, r2, AF.Exp, scale=-0.5)
                    yield
                    stt(oT(0, h), fa, sm[:, SM_GSUB:SM_GSUB + 1], r2, ALU.mult, ALU.mult)

                drain()
                g = a_tail()
                next(g)
                tails.append(g)


        a_items = [(h, j) for h in range(4) for j in range(nkb)]
        pend = []
        for item in a_items + [None, None]:
            if item is not None:
                h, j = item
                jl = j - 4 * t
                c0 = max(jl, 0) * 128
                diag = jl >= 0
                S0 = ps_next()
                S1 = ps_next()
                for comp, Sx in ((0, S0), (1, S1)):
                    rows = slice(comp * 64, comp * 64 + 64)
                    mm(Sx[:, c0:T], akT[rows, h, j * 128:(j + 1) * 128], aqT(h)[rows, c0:T],
                       start=True, stop=not diag, skip_group_check=True)
                    if diag:
                        mm(Sx[:, c0:c0 + 128], C(CM_IDENT), C(CM_MASKA), start=False, stop=True,
                           skip_group_check=True)
                E0 = e_next()
                E1 = e_next()
                act(E0[:, c0:T], S0[:, c0:T], AF.Exp, scale=0.125)
                act(E1[:, c0:T], S1[:, c0:T], AF.Exp, scale=0.125)
                pend.append((h, j, c0, E0, E1))
            while len(pend) > 2 or (item is None and pend):
                ph, pj, pc0, pE0, pE1 = pend.pop(0)
                first = pj == 0
                last = pj == nkb - 1
                vj = avc[:, pj, ph * 128:(ph + 1) * 128]
                mm(o0[:, pc0:T], vj, pE0[:, pc0:T], start=first, stop=last, skip_group_check=True)
                mm(o1[:, pc0:T], vj, pE1[:, pc0:T], start=first, stop=last, skip_group_check=True)
                mm(s01[:, pc0:T], C(CM_ONESL), pE0[:, pc0:T], start=first, stop=False, skip_group_check=True)
                mm(s01[:, pc0:T], C(CM_ONESR), pE1[:, pc0:T], start=False, stop=last, skip_group_check=True)
                if last:
                    start_a_tail(ph)
            if item is not None:
                tick()

        ps_pool["banks"] = [4, 5, 6, 7]
        jlo = max(0, 4 * t - 4)
        jhi = 4 * t + 3

        def start_b_tail(c):
            O = ps[(c % 2) * 2]
            Ss = ps[(c % 2) * 2 + 1]

            def b_tail():
                r = f_next()
                act(r, Ss[:, :], AF.Ln)
                yield
                act(r, r, AF.Exp, scale=-1.0)
                yield
                tt(oT(1, c), O[:, :], r, ALU.mult)

            g = b_tail()
            next(g)
            tails.append(g)

        b_items = [(c, j) for c in range(4) for j in range(jlo, jhi + 1)]
        pend = []
        for item in b_items + [None, None]:
            if item is not None:
                c, j = item
                qlo = max(j, 4 * t)
                qhi = min(j + 4, 4 * t + 3)
                c0 = (qlo - 4 * t) * 128
                c1 = (qhi - 4 * t + 1) * 128
                slot_k = j % 8
                Sbs = [ps_next(), ps_next()]
                for half in range(2):
                    rows = slice(half * 64, half * 64 + 64)
                    mm(Sbs[half][:, c0:c1], bkT[rows, c, slot_k * 128:(slot_k + 1) * 128], bqT(c)[rows, c0:c1],
                       start=True, stop=True)
                Es = [e_next(), e_next()]
                for half in range(2):
                    act(Es[half][:, c0:c1], Sbs[half][:, c0:c1], AF.Exp, scale=0.125)
                for half in range(2):
                    tt(Es[half][:, c0:c1], Es[half][:, c0:c1],
                       BT[:, 2 * c + half, (qlo - j) * 128:(qhi - j + 1) * 128], ALU.mult)
                pend.append((c, j, c0, c1, Es))
                tick()
            while len(pend) > 1 or (item is None and pend):
                pc, pj, pc0, pc1, pEs = pend.pop(0)
                O = ps[(pc % 2) * 2]
                Ss = ps[(pc % 2) * 2 + 1]
                sk = pj % 8
                for phalf in range(2):
                    phb = 2 * pc + phalf
                    prow = slice(phalf * 64, phalf * 64 + 64)
                    mm(O[prow, pc0:pc1], bvc[:, sk, phb * 64:(phb + 1) * 64], pEs[phalf][:, pc0:pc1],
                       start=(pj == jlo), stop=(pj == jhi), skip_group_check=True, tile_position=(0, phalf * 64))
                    mm(Ss[prow, pc0:pc1], C(CM_ONES, 64), pEs[phalf][:, pc0:pc1],
                       start=(pj == jlo), stop=(pj == jhi), skip_group_check=True, tile_position=(0, phalf * 64))
                if pj == jhi:
                    start_b_tail(pc)

        drain()
        CSCALE = 128.0 ** -0.5

        def start_c_tail(h):
            O = ps[(h % 2) * 2]
            Ss = ps[(h % 2) * 2 + 1]

            def c_tail():
                r = f_next()
                act(r, Ss[:, :], AF.Ln)
                yield
                act(r, r, AF.Exp, scale=-1.0)
                yield
                tt(oT(2, h), O[:, :], r, ALU.mult)

            g = c_tail()
            next(g)
            tails.append(g)

        pend = []
        for h in list(range(4)) + [None]:
            if h is not None:
                Es = []
                for mb in range(2):
                    Sc = ps_next()
                    mm(Sc[:, :], mkT[:, h, mb * 128:(mb + 1) * 128], cqT(h))
                    E = e_next()
                    act(E, Sc[:, :], AF.Exp, scale=CSCALE)
                    Es.append(E)
                pend.append((h, Es))
                tick()
            while len(pend) > 1 or (h is None and pend):
                ph, pEs = pend.pop(0)
                O = ps[(ph % 2) * 2]
                Ss = ps[(ph % 2) * 2 + 1]
                if ph >= 2:
                    drain()
                for mb in range(2):
                    mm(O[:, :], mvc[:, mb, ph * 128:(ph + 1) * 128], pEs[mb], start=(mb == 0), stop=(mb == 1))
                    mm(Ss[:, :], C(CM_ONES), pEs[mb], start=(mb == 0), stop=(mb == 1))
                start_c_tail(ph)

        drain()
        ps_pool["banks"] = [0, 1, 2, 3, 4, 5, 6]
        br_slot = None
        for dc in range(8):
            slot_g = use_unit(f"gate{dc}")
            if dc % 2 == 0:
                br_slot = use_unit(f"br{dc // 2}")
            gts = []
            for n in range(3):
                pg = ps_next()
                for kc in range(8):
                    mm(pg[:, :], Wv(slot_g, (n * 8 + kc) * 128, 128), hT[:, kc, :],
                       start=(kc == 0), stop=(kc == 7))
                g = e_next()
                gf = f_next()
                act(gf, pg[:, :], AF.Exp, bias=smb[:, dc * 3 + n:dc * 3 + n + 1], scale=-1.0)
                act(gf, gf, AF.Ln, bias=one_ap)
                act(g, gf, AF.Exp, scale=-1.0)
                gts.append(g)
            pbs = []
            for n in range(3):
                pb = ps_next()
                for kc in range(4):
                    mm(pb[:, :], Wv(br_slot, (((dc % 2) * 3 + n) * 4 + kc) * 128, 128), oT(n, kc),
                       start=(kc == 0), stop=(kc == 3))
                pbs.append(pb)
            a = f_next()
            b = f_next()
            tt(a, gts[0], pbs[0][:, :], ALU.mult)
            tt(b, gts[1], pbs[1][:, :], ALU.mult)
            tt(a, a, b, ALU.add)
            tt(b, gts[2], pbs[2][:, :], ALU.mult)
            tt(yT(dc), a, b, ALU.add)
        for g2 in range(2):
            slot = use_unit(f"wo{g2}")
            for dl in range(4):
                dc = g2 * 4 + dl
                po = ps_next()
                for kc in range(8):
                    mm(po[:, :], Wv(slot, (dl * 8 + kc) * 128, 128), yT(kc), start=(kc == 0), stop=(kc == 7))
                tt(x_sb[:, dc, :], po[:, :], x_sb[:, dc, :], ALU.add)
                if dc >= 1:
                    fused_sq(dc - 1)
        fused_sq(7)

        finish_norm(PV_FFN2, lambda kc: hT[:, kc, :])
        nxt = t + 1 if t + 1 < NT else None
        ffn(2, prefetch_tile=nxt)
        finish_norm(PV_FIN, lambda kc: Uf[:, kc, :], r=rnext[:, :])
        if nxt is not None:
            for k2 in range(3):
                dma(BQ, x_sb[:, 2 * k2:2 * k2 + 2, :], ftmp[:, 2 * k2:2 * k2 + 2, :], f"xc{k2}",
                    reads=[ftmp[:, 2 * k2:2 * k2 + 2, :]], writes=[x_sb[:, 2 * k2:2 * k2 + 2, :]])
            dma(BQ, x_sb[:, 6, :], cos_sb[:, :], "xc3", reads=[cos_sb[:, :]], writes=[x_sb[:, 6, :]])
            dma(BQ, x_sb[:, 7, :], sin_sb[:, :], "xc4", reads=[sin_sb[:, :]], writes=[x_sb[:, 7, :]])
        dma(BQ, outT.rearrange("(k p) s -> p k s", p=128)[:, :, c0t:c0t + T], Uf[:, :, :], "st",
            reads=[Uf[:, :, :]], writes=[("out", t)])

    sc.add("sp", None, reads=[("out", t) for t in range(NT)], writes=[])
    assert st["next_use"] == len(stream)
    sc.emit(nc, es)
    es.close()
    return nc


def make_core_inputs(inputs, b, S=SEQ, shared=None):
    x = np.asarray(inputs["x"])
    mem = np.asarray(inputs["mem"])
    d = dict(shared)
    d["xT"] = np.ascontiguousarray(x[b, :S].T)
    d["memT"] = np.ascontiguousarray(mem[b].T)
    return d


def make_shared(inputs, S=SEQ):
    inp = {k: np.asarray(v, dtype=np.float32) for k, v in inputs.items() if k not in ("x", "mem")}
    cmat, cosT, sinT = build_consts(S)
    return {
        "wall": build_wall(inp),
        "pvec": build_pvec(inp),
        "cmat": cmat,
        "cosT": cosT,
        "sinT": sinT,
        "btab": build_btab(inp["b_rel_bias"][0]),
    }


def kernel(**inputs):
    x = np.asarray(inputs["x"])
    B, S, _ = x.shape
    shared = make_shared(inputs, S)
    nc = build_program(S)
    in_maps = [make_core_inputs(inputs, b, S, shared) for b in range(B)]
    res = run_bass_kernel_spmd(nc, in_maps, core_ids=list(range(B)))
    out = np.empty((B, S, D), np.float32)
    for b in range(B):
        out[b] = np.asarray(res.results[b]["outT"]).T
    return out
```

```python
import math
from contextlib import ExitStack

import numpy as np
import concourse.bass as bass
import concourse.mybir as mybir
from concourse.bass_utils import run_bass_kernel_spmd

F32 = mybir.dt.float32
BF16 = mybir.dt.bfloat16
AF = mybir.ActivationFunctionType
ALU = mybir.AluOpType

D = 1024
SEQ = 4096
NMEM = 256
DFF = 2816
NFC = DFF // 128
T = 512
EPS = 1e-6
NSLOT = 4
SLOTW = 4096
GRAN = 256
SAME_ENGINE_SYNC = True
BQ = "pool"
SUMENG = "pool"
EW2 = "pool"
LAMBDA_INIT = 0.8 - 0.6 * math.exp(-0.3 * 0)

PV_FFN1, PV_MIX, PV_FFN2, PV_FIN, PV_MEMN = 0, 8, 16, 24, 32
PV_BG = 40
PV_AQ, PV_AK, PV_ASUB, PV_BQ, PV_BK, PV_CQ, PV_CK = 64, 65, 66, 67, 68, 69, 70
PV_LAM = 71
NPV = PV_LAM + 256
CM_INV1024, CM_BD64, CM_C128, CM_ONES, CM_IDENT, CM_ROT, CM_MASKA, CM_ONESL, CM_ONESR, CM_SEL0, CM_SEL1 = range(11)
NCM = 11
NEG = -30000.0


def _fm(w, c0, cw=128):
    k = w.shape[0]
    nk = k // 128
    return np.ascontiguousarray(
        w[:, c0:c0 + cw].reshape(nk, 128, cw).transpose(1, 0, 2).reshape(128, nk * cw))


def build_units():
    units = []
    units.append(("mk", 4096))
    units.append(("mv", 4096))
    for f in (1, 2):
        for i in range(11):
            units.append((f"ffn{f}_up{i}", 4096))
        for dc in range(8):
            units.append((f"ffn{f}_dn{dc}", NFC * 128))
        if f == 1:
            for nm in ("ak", "av", "aq", "bk", "bv", "bq", "cq"):
                units.append((f"in_{nm}", 4096))
            for dc in range(8):
                units.append((f"gate{dc}", 3072))
                if dc % 2 == 0:
                    units.append((f"br{dc // 2}", 3072))
            units.append(("wo0", 4096))
            units.append(("wo1", 4096))
    offs = {}
    o = 0
    for nm, w in units:
        offs[nm] = (o, w)
        o += w
    return units, offs, o


UNITS, UOFF, WALLW = build_units()


def build_wall(inp):
    parts = {}
    wkv = inp["w_mem_kv"][0]
    parts["mk"] = np.concatenate([_fm(wkv, h * 128) for h in range(4)], axis=1)
    parts["mv"] = _fm(wkv, 512, 512)
    for f in (1, 2):
        wg, wu, wd = inp[f"ffn{f}_wg"][0], inp[f"ffn{f}_wu"][0], inp[f"ffn{f}_wd"][0]
        for i in range(11):
            parts[f"ffn{f}_up{i}"] = np.concatenate(
                [_fm(wg, (2 * i) * 128), _fm(wu, (2 * i) * 128),
                 _fm(wg, (2 * i + 1) * 128), _fm(wu, (2 * i + 1) * 128)], axis=1)
        for dc in range(8):
            parts[f"ffn{f}_dn{dc}"] = _fm(wd, dc * 128)
    win = inp["w_in"][0]
    parts["in_aq"] = np.concatenate([_fm(win, 0 + h * 128) for h in range(4)], axis=1)
    parts["in_ak"] = np.concatenate([_fm(win, 512 + h * 128) for h in range(4)], axis=1)
    parts["in_av"] = _fm(win, 1024, 512)
    parts["in_bq"] = np.concatenate([_fm(win, 1536 + h * 128) for h in range(4)], axis=1)
    parts["in_bk"] = np.concatenate([_fm(win, 2048 + h * 128) for h in range(4)], axis=1)
    parts["in_bv"] = _fm(win, 2560, 512)
    parts["in_cq"] = np.concatenate([_fm(win, 3072 + h * 128) for h in range(4)], axis=1)
    wgt = inp["w_gate"][0]
    wbr = inp["w_branch"][0]
    for dc in range(8):
        parts[f"gate{dc}"] = np.concatenate(
            [_fm(wgt, n * 1024 + dc * 128) for n in range(3)], axis=1)
    for pr in range(4):
        parts[f"br{pr}"] = np.concatenate(
            [_fm(wbr[n], dc * 128) for dc in (2 * pr, 2 * pr + 1) for n in range(3)], axis=1)
    wo = inp["w_out"][0]
    parts["wo0"] = np.concatenate([_fm(wo, dc * 128) for dc in range(4)], axis=1)
    parts["wo1"] = np.concatenate([_fm(wo, dc * 128) for dc in range(4, 8)], axis=1)
    wall = np.empty((128, WALLW), np.float32)
    for nm, w in UNITS:
        o, _ = UOFF[nm]
        assert parts[nm].shape == (128, w), (nm, parts[nm].shape, w)
        wall[:, o:o + w] = parts[nm]
    return wall


def build_pvec(inp):
    pv = np.zeros((128, NPV), np.float32)

    def colmajor(v):
        return v.reshape(8, 128).T

    pv[:, PV_FFN1:PV_FFN1 + 8] = colmajor(inp["ffn1_norm"][0])
    pv[:, PV_MIX:PV_MIX + 8] = colmajor(inp["mix_norm"][0])
    pv[:, PV_FFN2:PV_FFN2 + 8] = colmajor(inp["ffn2_norm"][0])
    pv[:, PV_FIN:PV_FIN + 8] = colmajor(inp["final_norm"][0])
    pv[:, PV_MEMN:PV_MEMN + 8] = colmajor(inp["mem_norm"][0])
    bg = inp["b_gate"][0]
    for dc in range(8):
        for n in range(3):
            pv[:, PV_BG + dc * 3 + n] = bg[n * 1024 + dc * 128: n * 1024 + dc * 128 + 128]
    p = np.arange(128)
    pv[:, PV_AQ] = inp["a_q_norm"][0][p % 64]
    pv[:, PV_AK] = inp["a_k_norm"][0][p % 64]
    pv[:, PV_ASUB] = inp["a_sub_norm"][0]
    pv[:, PV_BQ] = inp["b_q_norm"][0][p % 64]
    pv[:, PV_BK] = inp["b_k_norm"][0][p % 64]
    pv[:, PV_CQ] = inp["c_q_norm"][0]
    pv[:, PV_CK] = inp["c_k_norm"][0]
    pv[:, PV_LAM:PV_LAM + 256] = inp["a_lambda"][0].reshape(1, 256)
    return pv


def build_consts(S):
    cm = np.zeros((128, NCM, 128), np.float32)
    cm[:, CM_INV1024] = 1.0 / 1024
    for b in range(2):
        cm[b * 64:(b + 1) * 64, CM_BD64, b * 64:(b + 1) * 64] = 1.0 / 64
    cm[:, CM_C128] = 1.0 / 128
    cm[:, CM_ONES] = 1.0
    cm[:, CM_IDENT] = np.eye(128, dtype=np.float32)
    for m in range(128):
        if (m % 64) < 32:
            cm[m + 32, CM_ROT, m] = -1.0
        else:
            cm[m - 32, CM_ROT, m] = 1.0
    k = np.arange(128)[:, None]
    q = np.arange(128)[None, :]
    cm[:, CM_MASKA] = np.where((k // 64) > (q // 64), NEG, 0.0)
    cm[:, CM_ONESL, 0:64] = 1.0
    cm[:, CM_ONESR, 64:128] = 1.0
    cm[0:64, CM_SEL0, :] = 1.0 / 64
    cm[64:128, CM_SEL1, :] = 1.0 / 64
    half = 32
    inv = (10000.0 ** (-np.arange(half, dtype=np.float32) / half)).astype(np.float32)
    pos = np.arange(S, dtype=np.float32)
    ang = (pos[:, None] * inv[None, :]).astype(np.float32)
    i_of_p = (np.arange(128) % 64) % 32
    cosT = np.ascontiguousarray(np.cos(ang).astype(np.float32)[:, i_of_p].T)
    sinT = np.ascontiguousarray(np.sin(ang).astype(np.float32)[:, i_of_p].T)
    return cm.reshape(128, NCM * 128), cosT, sinT


def build_btab(rel):
    k = np.arange(128)[:, None]
    c = np.arange(640)[None, :]
    idx = np.clip(c - k, -128, 128) + 128
    bt = rel[:, idx]
    return np.ascontiguousarray(bt.transpose(1, 0, 2)).astype(np.float32)


_DSZ = {F32: 4, BF16: 2}


class _Op:
    __slots__ = ("eng", "fn", "waits", "epoch", "pos", "dma", "flag")


class Sched:
    ENGS = ("pe", "act", "dve", "pool", "sp")

    def __init__(self):
        self.ops = {e: [] for e in self.ENGS}
        self.known = {e: {} for e in self.ENGS}
        self.selfk = {e: 0 for e in self.ENGS}
        self.gran = {}
        self.clock = {}
        self.dma_cnt = {}
        self.epoch = 0
        self.flagged = set()

    @staticmethod
    def keys(ap):
        if isinstance(ap, tuple):
            return [ap]
        t = ap.tensor
        name = t.name
        if name.startswith("psb"):
            return [(name,)]
        esz = _DSZ[ap.dtype]
        shp = tuple(t.shape)
        pstride = 1
        for s in shp[1:]:
            pstride *= s
        off = int(ap.offset) % pstride
        dims = ap.ap
        hi = off
        for st, cnt in dims[1:]:
            hi += (cnt - 1) * st
        lo_b = off * esz
        hi_b = (hi + 1) * esz
        return [(name, g) for g in range(lo_b // GRAN, (hi_b - 1) // GRAN + 1)]

    def add(self, eng, fn, reads=(), writes=(), dma=None):
        deps = []
        rk = []
        for r in reads:
            rk.extend(self.keys(r))
        wk = []
        for w in writes:
            wk.extend(self.keys(w))
        for g in rk:
            st = self.gran.get(g)
            if st is not None and st[0] is not None:
                deps.append(st[0])
        for g in wk:
            st = self.gran.get(g)
            if st is not None:
                if st[0] is not None:
                    deps.append(st[0])
                deps.extend(st[1])
        op = _Op()
        op.eng = eng
        op.fn = fn
        op.epoch = self.epoch
        op.dma = None
        op.flag = False
        known = self.known[eng]
        waits = {}
        for ev in sorted(set(deps), key=lambda z: (z[0], str(z[1]), z[2]), reverse=True):
            kind, key, val = ev
            if kind == "c" and key == eng:
                if eng == "pe" or eng == "sp" or not SAME_ENGINE_SYNC:
                    continue
                if self.selfk[eng] >= val:
                    continue
                self.selfk[eng] = val
                if waits.get((kind, key), 0) < val:
                    waits[(kind, key)] = val
                continue
            kk = key if kind == "c" else ("d", key)
            if known.get(kk, 0) >= val:
                continue
            if waits.get((kind, key), 0) < val:
                waits[(kind, key)] = val
        for (kind, key), val in waits.items():
            ck = self.clock[(kind, key, val)]
            for a, b in ck.items():
                if known.get(a, 0) < b:
                    known[a] = b
            if kind == "c":
                self.flagged.add((key, val))
        op.waits = list(waits.items())
        self.ops[eng].append(op)
        op.pos = len(self.ops[eng])
        if dma is None:
            ev = ("c", eng, op.pos)
            ck = dict(known)
            ck[eng] = op.pos
            known[eng] = op.pos
        else:
            cnt = self.dma_cnt.get(dma, 0) + 16
            self.dma_cnt[dma] = cnt
            op.dma = (dma, cnt)
            ev = ("d", dma, cnt)
            ck = dict(known)
            ck[("d", dma)] = cnt
        self.clock[ev] = ck
        for g in rk:
            st = self.gran.setdefault(g, [None, []])
            lst = st[1]
            if ev[0] == "c":
                lst[:] = [e for e in lst if not (e[0] == "c" and e[1] == ev[1])]
            lst.append(ev)
        for g in wk:
            self.gran[g] = [ev, []]
        return op

    def emit(self, nc, es):
        groups = {}
        for e in self.ENGS:
            for op in self.ops[e]:
                if op.dma is None and (e, op.pos) in self.flagged:
                    op.flag = True
                    groups.setdefault((e, op.epoch), []).append(op.pos)
        sem = {}
        rank = {}
        for (e, ep), lst in groups.items():
            sem[(e, ep)] = es.enter_context(nc.semaphore(f"s_{e}_{ep}"))
            for i, p in enumerate(lst):
                rank[(e, p)] = (sem[(e, ep)], i + 1)
        dsem = {}
        for nm in self.dma_cnt:
            dsem[nm] = es.enter_context(nc.semaphore(f"d_{nm}"))
        block = es.enter_context(nc.Block())

        def run(engobj, e):
            for op in self.ops[e]:
                for (kind, key), val in op.waits:
                    if kind == "c":
                        s, v = rank[(key, val)]
                    else:
                        s, v = dsem[key], val
                    engobj.wait_ge(s, v)
                if op.fn is None:
                    continue
                inst = op.fn(engobj)
                if op.dma is not None:
                    inst.then_inc(dsem[op.dma[0]], 16)
                elif op.flag:
                    s, _ = rank[(e, op.pos)]
                    inst.then_inc(s, 1)

        @block.tensor
        def _(eng):
            run(eng, "pe")

        @block.scalar
        def _(eng):
            run(eng, "act")

        @block.vector
        def _(eng):
            run(eng, "dve")

        @block.gpsimd
        def _(eng):
            run(eng, "pool")

        @block.sync
        def _(eng):
            run(eng, "sp")


def build_program(S=SEQ, debug=False):
    NT = S // T
    NKB = S // 128
    nc = bass.Bass("TRN2", target_bir_lowering=False)
    xT = nc.dram_tensor("xT", [D, S], F32, kind="ExternalInput").ap()
    memT = nc.dram_tensor("memT", [D, NMEM], F32, kind="ExternalInput").ap()
    wall = nc.dram_tensor("wall", [128, WALLW], F32, kind="ExternalInput").ap()
    pvec_d = nc.dram_tensor("pvec", [128, NPV], F32, kind="ExternalInput").ap()
    cmat_d = nc.dram_tensor("cmat", [128, NCM * 128], F32, kind="ExternalInput").ap()
    cos_d = nc.dram_tensor("cosT", [128, S], F32, kind="ExternalInput").ap()
    sin_d = nc.dram_tensor("sinT", [128, S], F32, kind="ExternalInput").ap()
    btab_d = nc.dram_tensor("btab", [128, 8, 640], F32, kind="ExternalInput").ap()
    outT = nc.dram_tensor("outT", [D, S], F32, kind="ExternalOutput").ap()
    scr = nc.dram_tensor("scr", [128, WALLW], BF16, kind="Internal").ap()

    sc = Sched()
    es = ExitStack()

    def sb(name, shape, dt):
        return es.enter_context(nc.sbuf_tensor(name, shape, dt))

    akT = sb("akT", [128, 4, S], BF16)
    avc = sb("avc", [128, NKB, 512], BF16)
    bkT = sb("bkT", [128, 4, 1024], BF16)
    bvc = sb("bvc", [128, 8, 512], BF16)
    x_sb = sb("x_sb", [128, 8, T], F32)
    hT = sb("hT", [128, 8, T], BF16)
    U = sb("U", [128, 24, T], BF16)
    wring = sb("wring", [128, NSLOT, SLOTW], BF16)
    cos_sb = sb("cos_sb", [128, T], F32)
    sin_sb = sb("sin_sb", [128, T], F32)
    BT = sb("BT", [128, 8, 640], BF16)
    mkT = sb("mkT", [128, 4, NMEM], BF16)
    mvc = sb("mvc", [128, 2, 512], BF16)
    cm = sb("cm", [128, NCM, 128], BF16)
    pv = sb("pv", [128, NPV], F32)
    sm = sb("sm", [128, 16], F32)
    sqh = sb("sqh", [128, 2, T], BF16)
    ftmp = sb("ftmp", [128, 6, T], F32)
    etmp = sb("etmp", [128, 6, T], BF16)
    qn = sb("qn", [128, 2, T], BF16)
    rnext = sb("rnext", [128, T], F32)
    ps = [es.enter_context(nc.psum_tensor(f"psb{i}", [128, 512], F32)) for i in range(8)]

    def hid(fc):
        return U[:, fc, :]

    SM_NEGLAM, SM_GSUB, SM_EPS, SM_HB0 = 0, 1, 2, 3

    smb = sb("smb", [128, 24], F32)

    cnt = {"ps": 0, "f": 0, "e": 0, "q": 0, "s": 0}
    ps_pool = {"banks": list(range(8))}

    def ps_next():
        b = ps_pool["banks"][cnt["ps"] % len(ps_pool["banks"])]
        cnt["ps"] += 1
        return ps[b]

    def f_next():
        i = cnt["f"] % 6
        cnt["f"] += 1
        return ftmp[:, i, :]

    def e_next():
        i = cnt["e"] % 6
        cnt["e"] += 1
        return etmp[:, i, :]

    def q_next():
        i = cnt["q"] % 2
        cnt["q"] += 1
        return qn[:, i, :]

    def s_next():
        i = cnt["s"] % 2
        cnt["s"] += 1
        return sqh[:, i, :]

    def C(i, m=128):
        return cm[:, i, 0:m]

    def mm(out, lhsT, rhs, start=True, stop=True, **kw):
        sc.add("pe", lambda e: e.matmul(out, lhsT, rhs, start=start, stop=stop, **kw),
               reads=[lhsT, rhs], writes=[out])

    def act(out, in_, func, bias=None, scale=1.0):
        rd = [in_]
        if bias is not None and not isinstance(bias, float):
            rd.append(bias)
        if bias is None:
            sc.add("act", lambda e: e.activation(out=out, in_=in_, func=func, scale=scale),
                   reads=rd, writes=[out])
        else:
            sc.add("act", lambda e: e.activation(out=out, in_=in_, func=func, bias=bias, scale=scale),
                   reads=rd, writes=[out])

    def tt(out, in0, in1, op, eng="dve"):
        sc.add(eng, lambda e: e.tensor_tensor(out=out, in0=in0, in1=in1, op=op),
               reads=[in0, in1], writes=[out])

    def stt(out, in0, scalar, in1, op0, op1, eng="dve"):
        rd = [in0, in1]
        if not isinstance(scalar, float):
            rd.append(scalar)
        sc.add(eng, lambda e: e.scalar_tensor_tensor(out=out, in0=in0, scalar=scalar, in1=in1,
                                                     op0=op0, op1=op1),
               reads=rd, writes=[out])

    def ts(out, in0, s1, op0, eng="dve"):
        rd = [in0]
        if not isinstance(s1, float):
            rd.append(s1)
        sc.add(eng, lambda e: e.tensor_scalar(out=out, in0=in0, scalar1=s1, scalar2=None, op0=op0),
               reads=rd, writes=[out])

    def recip(out, in_):
        act(out, in_, AF.Ln)
        act(out, out, AF.Exp, scale=-1.0)

    def memset(ap, val, eng="dve"):
        sc.add(eng, lambda e: e.memset(ap, val), reads=[], writes=[ap])

    def dma(eng, out, in_, sem, reads=(), writes=()):
        sc.add(eng, lambda e: e.dma_start(out=out, in_=in_), reads=list(reads), writes=list(writes), dma=sem)

    stream = []
    stream += ["mk", "mv"]
    per_tile = []
    per_tile += [f"ffn1_up{i}" for i in range(11)] + [f"ffn1_dn{dc}" for dc in range(8)]
    per_tile += ["in_ak", "in_av", "in_aq", "in_bk", "in_bv", "in_bq", "in_cq"]
    for dc in range(8):
        per_tile.append(f"gate{dc}")
        if dc % 2 == 0:
            per_tile.append(f"br{dc // 2}")
    per_tile += ["wo0", "wo1"]
    per_tile += [f"ffn2_up{i}" for i in range(11)] + [f"ffn2_dn{dc}" for dc in range(8)]
    for t in range(NT):
        stream += per_tile
    st = {"next_load": 0, "next_use": 0}

    UIDX = {nm: k for k, (nm, w) in enumerate(UNITS)}

    def load_unit(idx):
        nm = stream[idx]
        o, w = UOFF[nm]
        slot = idx % NSLOT
        if idx < len(UNITS):
            dma("pool", wring[:, slot, 0:w], wall[:, o:o + w], f"w{slot}", writes=[wring[:, slot, 0:w]])
            dma("sp", scr[:, o:o + w], wring[:, slot, 0:w], f"sw{slot}",
                reads=[wring[:, slot, 0:w]], writes=[("scr", UIDX[nm])])
        else:
            dma("sp", wring[:, slot, 0:w], scr[:, o:o + w], f"w{slot}",
                reads=[("scr", UIDX[nm])], writes=[wring[:, slot, 0:w]])

    def use_unit(nm):
        i = st["next_use"]
        assert stream[i] == nm, (stream[i], nm)
        st["next_use"] += 1
        while st["next_load"] < min(len(stream), i + NSLOT - 1):
            load_unit(st["next_load"])
            st["next_load"] += 1
        return i % NSLOT

    def Wv(slot, off, n):
        return wring[:, slot, off:off + n]

    dma("pool", cm[:, :, :], cmat_d.rearrange("p (a b) -> p a b", a=NCM), "const", writes=[cm[:, :, :]])
    dma("sp", pv[:, :], pvec_d[:, :], "pvl", writes=[pv[:, :]])
    memset(sm[:, SM_EPS:SM_EPS + 1], EPS)
    ts(sm[:, SM_GSUB:SM_GSUB + 1], pv[:, PV_ASUB:PV_ASUB + 1], 1.0 - LAMBDA_INIT, ALU.mult)
    ts(smb[:, :], pv[:, PV_BG:PV_BG + 24], -1.0, ALU.mult)
    memset(sm[:, 8:9], 1.0)
    lamt = ftmp[:, 0, 0:128]
    tt(lamt[:, 0:64], pv[:, PV_LAM:PV_LAM + 64], pv[:, PV_LAM + 64:PV_LAM + 128], ALU.mult)
    tt(lamt[:, 64:128], pv[:, PV_LAM + 128:PV_LAM + 192], pv[:, PV_LAM + 192:PV_LAM + 256], ALU.mult)
    sc.add("dve", lambda e: e.reduce_sum(out=sm[:, 4:5], in_=lamt[:, 0:64], axis=mybir.AxisListType.X),
           reads=[lamt[:, 0:64]], writes=[sm[:, 4:5]])
    sc.add("dve", lambda e: e.reduce_sum(out=sm[:, 5:6], in_=lamt[:, 64:128], axis=mybir.AxisListType.X),
           reads=[lamt[:, 64:128]], writes=[sm[:, 5:6]])
    act(sm[:, 6:8], sm[:, 4:6], AF.Exp)
    stt(sm[:, SM_NEGLAM:SM_NEGLAM + 1], sm[:, 7:8], -LAMBDA_INIT, sm[:, 6:7], ALU.add, ALU.subtract)
    eps_ap = sm[:, SM_EPS:SM_EPS + 1]
    one_ap = sm[:, 8:9]

    xflat = x_sb[:, :, :].rearrange("p a b -> p (a b)")
    for hh in range(2):
        stg = xflat[:, 0:2560].rearrange("p (h c) -> p h c", h=4)
        dma("sp", stg, btab_d[:, hh * 4:(hh + 1) * 4, :], "bt", writes=[stg])
        sc.add("act", lambda e, stg=stg, hh=hh: e.mul(out=BT[:, hh * 4:(hh + 1) * 4, :], in_=stg, mul=8.0),
               reads=[stg], writes=[BT[:, hh * 4:(hh + 1) * 4, :]])
    memset(BT[64:128, :, 0:64], NEG)
    memset(BT[0:64, :, 4 * 128 + 64:4 * 128 + 128], NEG)
    act(BT[:, :, :], BT[:, :, :], AF.Exp, scale=0.125)

    def rmsnorm_x(src, n, gcol, dst):
        sq = U[:, 0:8, 0:n]
        act(U[:, 0:4, 0:n], src[:, 0:4, 0:n], AF.Square)
        act(U[:, 4:8, 0:n], src[:, 4:8, 0:n], AF.Square)
        p = ps_next()
        for kc in range(8):
            mm(p[:, 0:n], C(CM_INV1024), U[:, kc, 0:n], start=(kc == 0), stop=(kc == 7))
        r = f_next()
        act(r[:, 0:n], p[:, 0:n], AF.Ln, bias=eps_ap)
        act(r[:, 0:n], r[:, 0:n], AF.Exp, scale=-0.5)
        for kc in range(8):
            stt(dst[:, kc, 0:n], src[:, kc, 0:n], pv[:, gcol + kc:gcol + kc + 1], r[:, 0:n],
                ALU.mult, ALU.mult)

    xsrc = xT.rearrange("(k p) s -> p k s", p=128)
    NB = 7

    def fused_sq(dc):
        sqb = U[:, 22 + (dc % 2), :]
        act(sqb, x_sb[:, dc, :], AF.Square)
        mm(ps[NB][:, :], C(CM_INV1024), sqb, start=(dc == 0), stop=(dc == 7), skip_group_check=True)

    def finish_norm(gcol, dst_of_kc, r=None):
        if r is None:
            r = f_next()
        act(r, ps[NB][:, :], AF.Ln, bias=eps_ap)
        act(r, r, AF.Exp, scale=-0.5)
        for kc in range(8):
            stt(dst_of_kc(kc), x_sb[:, kc, :], pv[:, gcol + kc:gcol + kc + 1], r, ALU.mult, ALU.mult)

    Uf = U[:, 8:24, :].bitcast(F32).rearrange("p (k a) c -> p k (a c)", a=2)

    NB2 = 6

    def xstage(kc):
        if kc < 6:
            return ftmp[:, kc, :]
        return cos_sb[:, :] if kc == 6 else sin_sb[:, :]

    def sqstage(kc):
        return etmp[:, kc, :] if kc < 6 else qn[:, kc - 6, :]

    def prefetch_x(tn):
        dma(BQ, ftmp[:, 0:6, :], xsrc[:, 0:6, tn * T:(tn + 1) * T], "xs0", writes=[ftmp[:, 0:6, :]])
        dma(BQ, cos_sb[:, :], xsrc[:, 6, tn * T:(tn + 1) * T], "xs1", writes=[cos_sb[:, :]])
        dma(BQ, sin_sb[:, :], xsrc[:, 7, tn * T:(tn + 1) * T], "xs2", writes=[sin_sb[:, :]])

    def next_norm_part1():
        for kc in range(8):
            if kc % 2 == 0:
                act(sqstage(kc), xstage(kc), AF.Square)
            else:
                tt(sqstage(kc), xstage(kc), xstage(kc), ALU.mult)
            mm(ps[NB2][:, :], C(CM_INV1024), sqstage(kc), start=(kc == 0), stop=(kc == 7), skip_group_check=True)
        act(rnext[:, :], ps[NB2][:, :], AF.Ln, bias=eps_ap)
        act(rnext[:, :], rnext[:, :], AF.Exp, scale=-0.5)

    def next_norm_part2(k0, k1):
        for kc in range(k0, k1):
            stt(hT[:, kc, :], xstage(kc), pv[:, PV_FFN1 + kc:PV_FFN1 + kc + 1], rnext[:, :], ALU.mult, ALU.mult)


    def head_rstd(pq, n, cmi):
        s = s_next()
        act(s[:, 0:n], pq[:, 0:n], AF.Square)
        p = ps_next()
        mm(p[:, 0:n], C(cmi), s[:, 0:n])
        r = f_next()
        act(r[:, 0:n], p[:, 0:n], AF.Ln, bias=eps_ap)
        act(r[:, 0:n], r[:, 0:n], AF.Exp, scale=-0.5)
        return r

    def proj_fm(slot, ci, n):
        p = ps_next()
        for kc in range(8):
            mm(p[:, 0:n], Wv(slot, (ci * 8 + kc) * 128, 128), hT[:, kc, 0:n],
               start=(kc == 0), stop=(kc == 7))
        return p

    def normed(pq, n, cmi, gcol, dst):
        r = head_rstd(pq, n, cmi)
        stt(dst, pq[:, 0:n], pv[:, gcol:gcol + 1], r[:, 0:n], ALU.mult, ALU.mult)

    def roped(pq, gcol, dst):
        r = head_rstd(pq, T, CM_BD64)
        q = q_next()
        stt(q, pq[:, :], pv[:, gcol:gcol + 1], r, ALU.mult, ALU.mult)
        pr = ps_next()
        mm(pr[:, :], C(CM_ROT), q)
        t1 = f_next()
        tt(t1, q, cos_sb[:, :], ALU.mult)
        t2 = f_next()
        tt(t2, pr[:, :], sin_sb[:, :], ALU.mult)
        tt(dst, t1, t2, ALU.add)

    dma("sp", x_sb[:, :, 0:NMEM], memT.rearrange("(k p) s -> p k s", p=128), "meml",
        writes=[x_sb[:, :, 0:NMEM]])
    rmsnorm_x(x_sb, NMEM, PV_MEMN, hT)
    slot = use_unit("mk")
    for h in range(4):
        pq = proj_fm(slot, h, NMEM)
        normed(pq, NMEM, CM_C128, PV_CK, mkT[:, h, :])
    slot = use_unit("mv")
    for mb in range(2):
        p = ps_next()
        for kc in range(8):
            mm(p[:, :], hT[:, kc, mb * 128:(mb + 1) * 128], Wv(slot, kc * 512, 512),
               start=(kc == 0), stop=(kc == 7))
        sc.add("act", lambda e, p=p, mb=mb: e.copy(out=mvc[:, mb, :], in_=p[:, :]),
               reads=[p[:, :]], writes=[mvc[:, mb, :]])

    def ffn(f, prefetch_tile=None):
        ps_pool["banks"] = [0, 1, 2, 3, 4, 5, 6] if prefetch_tile is None else [0, 1, 2, 3, 4, 5]
        for i in range(11):
            slot = use_unit(f"ffn{f}_up{i}")
            for j in range(2):
                fc = 2 * i + j
                pg = ps_next()
                for kc in range(8):
                    mm(pg[:, :], Wv(slot, ((2 * j) * 8 + kc) * 128, 128), hT[:, kc, :],
                       start=(kc == 0), stop=(kc == 7))
                pu = ps_next()
                for kc in range(8):
                    mm(pu[:, :], Wv(slot, ((2 * j + 1) * 8 + kc) * 128, 128), hT[:, kc, :],
                       start=(kc == 0), stop=(kc == 7))
                sg = f_next()
                act(sg, pg[:, :], AF.Exp, scale=-1.0)
                act(sg, sg, AF.Ln, bias=one_ap)
                act(sg, sg, AF.Exp, scale=-1.0)
                tt(sg, sg, pg[:, :], ALU.mult)
                tt(hid(fc), sg, pu[:, :], ALU.mult)
        for dc in range(8):
            slot = use_unit(f"ffn{f}_dn{dc}")
            if prefetch_tile is not None and dc == 1:
                prefetch_x(prefetch_tile)
            pd = ps_next()
            for fc in range(NFC):
                mm(pd[:, :], Wv(slot, fc * 128, 128), hid(fc), start=(fc == 0), stop=(fc == NFC - 1))
            stt(x_sb[:, dc, :], pd[:, :], 0.5, x_sb[:, dc, :], ALU.mult, ALU.add)
            if dc >= 1:
                fused_sq(dc - 1)
            if prefetch_tile is not None:
                if dc == 4:
                    next_norm_part1()
                elif dc == 5:
                    next_norm_part2(0, 4)
                elif dc == 6:
                    next_norm_part2(4, 8)
        fused_sq(7)

    aqT = lambda h: U[:, 0 + h, :]
    bqT = lambda c: U[:, 4 + c, :]
    cqT = lambda h: U[:, 8 + h, :]
    oT = lambda n, k: U[:, 12 + 4 * n + k, :]
    yT = lambda k: U[:, k, :]

    tails = []

    def tick():
        for g in list(tails):
            try:
                next(g)
            except StopIteration:
                tails.remove(g)

    def drain():
        while tails:
            tick()

    def load_x(tn):
        for hf in range(2):
            dma("sp", x_sb[:, 4 * hf:4 * hf + 4, :], xsrc[:, 4 * hf:4 * hf + 4, tn * T:(tn + 1) * T], f"xl{hf}",
                writes=[x_sb[:, 4 * hf:4 * hf + 4, :]])

    for t in range(NT):
        sc.epoch = t + 1
        c0t = t * T
        ps_pool["banks"] = list(range(8))
        cnt["f"] = 0
        if t == 0:
            load_x(0)
        rq = "sp" if t == 0 else BQ
        dma(rq, cos_sb[:, :], cos_d[:, c0t:c0t + T], "ropec", writes=[cos_sb[:, :]])
        dma(rq, sin_sb[:, :], sin_d[:, c0t:c0t + T], "ropes", writes=[sin_sb[:, :]])

        ps_pool["banks"] = [0, 1, 2, 3, 4, 5, 6]
        if t == 0:
            for kc in range(8):
                if kc % 2 == 0:
                    act(U[:, kc, :], x_sb[:, kc, :], AF.Square)
                else:
                    tt(U[:, kc, :], x_sb[:, kc, :], x_sb[:, kc, :], ALU.mult)
                mm(ps[NB][:, :], C(CM_INV1024), U[:, kc, :], start=(kc == 0), stop=(kc == 7), skip_group_check=True)
            finish_norm(PV_FFN1, lambda kc: hT[:, kc, :])
        ffn(1)

        finish_norm(PV_MIX, lambda kc: hT[:, kc, :])
        ps_pool["banks"] = list(range(8))

        def job_fm(slot, ci, cmi, gcol, dst, rope):
            def gen():
                pq = proj_fm(slot, ci, T)
                sq_ = s_next()
                act(sq_, pq[:, :], AF.Square)
                yield
                p = ps_next()
                mm(p[:, :], C(cmi), sq_)
                r = f_next()
                act(r, p[:, :], AF.Ln, bias=eps_ap)
                act(r, r, AF.Exp, scale=-0.5)
                if not rope:
                    stt(dst, pq[:, :], pv[:, gcol:gcol + 1], r, ALU.mult, ALU.mult)
                    return
                q = q_next()
                stt(q, pq[:, :], pv[:, gcol:gcol + 1], r, ALU.mult, ALU.mult)
                yield
                pr = ps_next()
                mm(pr[:, :], C(CM_ROT), q)
                t1 = f_next()
                tt(t1, q, cos_sb[:, :], ALU.mult, eng=EW2)
                t2 = f_next()
                tt(t2, pr[:, :], sin_sb[:, :], ALU.mult)
                tt(dst, t1, t2, ALU.add)
            return gen

        def job_tm(slot, tb, dst):
            def gen():
                p = ps_next()
                for kc in range(8):
                    mm(p[:, :], hT[:, kc, tb * 128:(tb + 1) * 128], Wv(slot, kc * 512, 512),
                       start=(kc == 0), stop=(kc == 7))
                sc.add("act", lambda e: e.copy(out=dst, in_=p[:, :]), reads=[p[:, :]], writes=[dst])
                return
                yield
            return gen

        def run_jobs(jobs):
            active = []
            for jf in list(jobs) + [None, None, None]:
                prev = active
                active = []
                if jf is not None:
                    g = jf()
                    try:
                        next(g)
                        active.append(g)
                    except StopIteration:
                        pass
                for g in prev:
                    try:
                        next(g)
                        active.append(g)
                    except StopIteration:
                        pass

        jobs = []

        def lazy_unit(nm, mk):
            box = {}

            def wrap(i):
                def jf():
                    if "slot" not in box:
                        box["slot"] = use_unit(nm)
                    return mk(box["slot"], i)()
                return jf
            return wrap

        w = lazy_unit("in_ak", lambda sl, h: job_fm(sl, h, CM_BD64, PV_AK, akT[:, h, c0t:c0t + T], True))
        jobs += [w(h) for h in range(4)]
        w = lazy_unit("in_av", lambda sl, tb: job_tm(sl, tb, avc[:, t * 4 + tb, :]))
        jobs += [w(tb) for tb in range(4)]
        w = lazy_unit("in_aq", lambda sl, h: job_fm(sl, h, CM_BD64, PV_AQ, aqT(h), True))
        jobs += [w(h) for h in range(4)]
        w = lazy_unit("in_bk", lambda sl, c: job_fm(sl, c, CM_BD64, PV_BK, bkT[:, c, (t % 2) * T:(t % 2) * T + T], False))
        jobs += [w(c) for c in range(4)]
        w = lazy_unit("in_bv", lambda sl, tb: job_tm(sl, tb, bvc[:, (t * 4 + tb) % 8, :]))
        jobs += [w(tb) for tb in range(4)]
        w = lazy_unit("in_bq", lambda sl, c: job_fm(sl, c, CM_BD64, PV_BQ, bqT(c), False))
        jobs += [w(c) for c in range(4)]
        w = lazy_unit("in_cq", lambda sl, h: job_fm(sl, h, CM_C128, PV_CQ, cqT(h), False))
        jobs += [w(h) for h in range(4)]
        run_jobs(jobs)

        nkb = 4 * t + 4
        ps_pool["banks"] = [3, 4, 5, 6, 7]
        o0, o1, s01 = ps[0], ps[1], ps[2]

        def start_a_tail(h):
                def a_tail(h=h):
                    fa = f_next()
                    fb = f_next()
                    r = f_next()
                    sc.add("act", lambda e: e.copy(out=fa, in_=ps[0][:, :]), reads=[ps[0][:, :]], writes=[fa])
                    sc.add("dve", lambda e: e.tensor_copy(out=fb, in_=ps[1][:, :]), reads=[ps[1][:, :]], writes=[fb])
                    act(r, ps[2][:, :], AF.Ln)
                    yield
                    act(r, r, AF.Exp, scale=-1.0)
                    yield
                    rhi = s_next()
                    rlo = s_next()
                    sc.add("dve", lambda e: e.tensor_copy(out=rhi, in_=r), reads=[r], writes=[rhi])
                    tt(rlo, r, rhi, ALU.subtract)
                    yield
                    pbs_ = []
                    for comp in range(2):
                        pbc = ps_next()
                        sel = C(CM_SEL0 if comp == 0 else CM_SEL1)
                        mm(pbc[:, :], sel, rhi, start=True, stop=False)
                        mm(pbc[:, :], sel, rlo, start=False, stop=True)
                        pbs_.append(pbc)
                    yield
                    tt(fa, fa, pbs_[0][:, :], ALU.mult)
                    tt(fb, fb, pbs_[1][:, :], ALU.mult)
                    yield
                    stt(fa, fb, sm[:, SM_NEGLAM:SM_NEGLAM + 1], fa, ALU.mult, ALU.add)
                    yield
                    sq_ = s_next()
                    act(sq_, fa, AF.Square)
                    yield
                    p = ps_next()
                    mm(p[:, :], C(CM_C128), sq_)
                    yield
                    r2 = r
                    act(r2, p[:, :], AF.Ln, bias=eps_ap)
                    yield
                    act(r2, r2, AF.Exp, scale=-0.5)
                    yield
                    stt(oT(0, h), fa, sm[:, SM_GSUB:SM_GSUB + 1], r2, ALU.mult, ALU.mult)

                drain()
                g = a_tail()
                next(g)
                tails.append(g)


        a_items = [(h, j) for h in range(4) for j in range(nkb)]
        pend = []
        for item in a_items + [None, None]:
            if item is not None:
                h, j = item
                jl = j - 4 * t
                c0 = max(jl, 0) * 128
                diag = jl >= 0
                S0 = ps_next()
                S1 = ps_next()
                for comp, Sx in ((0, S0), (1, S1)):
                    rows = slice(comp * 64, comp * 64 + 64)
                    mm(Sx[:, c0:T], akT[rows, h, j * 128:(j + 1) * 128], aqT(h)[rows, c0:T],
                       start=True, stop=not diag, skip_group_check=True)
                    if diag:
                        mm(Sx[:, c0:c0 + 128], C(CM_IDENT), C(CM_MASKA), start=False, stop=True,
                           skip_group_check=True)
                E0 = e_next()
                E1 = e_next()
                act(E0[:, c0:T], S0[:, c0:T], AF.Exp, scale=0.125)
                act(E1[:, c0:T], S1[:, c0:T], AF.Exp, scale=0.125)
                pend.append((h, j, c0, E0, E1))
            while len(pend) > 2 or (item is None and pend):
                ph, pj, pc0, pE0, pE1 = pend.pop(0)
                first = pj == 0
                last = pj == nkb - 1
                vj = avc[:, pj, ph * 128:(ph + 1) * 128]
                mm(o0[:, pc0:T], vj, pE0[:, pc0:T], start=first, stop=last, skip_group_check=True)
                mm(o1[:, pc0:T], vj, pE1[:, pc0:T], start=first, stop=last, skip_group_check=True)
                mm(s01[:, pc0:T], C(CM_ONESL), pE0[:, pc0:T], start=first, stop=False, skip_group_check=True)
                mm(s01[:, pc0:T], C(CM_ONESR), pE1[:, pc0:T], start=False, stop=last, skip_group_check=True)
                if last:
                    start_a_tail(ph)
            if item is not None:
                tick()

        ps_pool["banks"] = [4, 5, 6, 7]
        jlo = max(0, 4 * t - 4)
        jhi = 4 * t + 3

        def start_b_tail(c):
            O = ps[(c % 2) * 2]
            Ss = ps[(c % 2) * 2 + 1]

            def b_tail():
                r = f_next()
                act(r, Ss[:, :], AF.Ln)
                yield
                act(r, r, AF.Exp, scale=-1.0)
                yield
                tt(oT(1, c), O[:, :], r, ALU.mult)

            g = b_tail()
            next(g)
            tails.append(g)

        b_items = [(c, j) for c in range(4) for j in range(jlo, jhi + 1)]
        pend = []
        for item in b_items + [None, None]:
            if item is not None:
                c, j = item
                qlo = max(j, 4 * t)
                qhi = min(j + 4, 4 * t + 3)
                c0 = (qlo - 4 * t) * 128
                c1 = (qhi - 4 * t + 1) * 128
                slot_k = j % 8
                Sbs = [ps_next(), ps_next()]
                for half in range(2):
                    rows = slice(half * 64, half * 64 + 64)
                    mm(Sbs[half][:, c0:c1], bkT[rows, c, slot_k * 128:(slot_k + 1) * 128], bqT(c)[rows, c0:c1],
                       start=True, stop=True)
                Es = [e_next(), e_next()]
                for half in range(2):
                    act(Es[half][:, c0:c1], Sbs[half][:, c0:c1], AF.Exp, scale=0.125)
                for half in range(2):
                    tt(Es[half][:, c0:c1], Es[half][:, c0:c1],
                       BT[:, 2 * c + half, (qlo - j) * 128:(qhi - j + 1) * 128], ALU.mult)
                pend.append((c, j, c0, c1, Es))
                tick()
            while len(pend) > 1 or (item is None and pend):
                pc, pj, pc0, pc1, pEs = pend.pop(0)
                O = ps[(pc % 2) * 2]
                Ss = ps[(pc % 2) * 2 + 1]
                sk = pj % 8
                for phalf in range(2):
                    phb = 2 * pc + phalf
                    prow = slice(phalf * 64, phalf * 64 + 64)
                    mm(O[prow, pc0:pc1], bvc[:, sk, phb * 64:(phb + 1) * 64], pEs[phalf][:, pc0:pc1],
                       start=(pj == jlo), stop=(pj == jhi), skip_group_check=True, tile_position=(0, phalf * 64))
                    mm(Ss[prow, pc0:pc1], C(CM_ONES, 64), pEs[phalf][:, pc0:pc1],
                       start=(pj == jlo), stop=(pj == jhi), skip_group_check=True, tile_position=(0, phalf * 64))
                if pj == jhi:
                    start_b_tail(pc)

        drain()
        CSCALE = 128.0 ** -0.5

        def start_c_tail(h):
            O = ps[(h % 2) * 2]
            Ss = ps[(h % 2) * 2 + 1]

            def c_tail():
                r = f_next()
                act(r, Ss[:, :], AF.Ln)
                yield
                act(r, r, AF.Exp, scale=-1.0)
                yield
                tt(oT(2, h), O[:, :], r, ALU.mult)

            g = c_tail()
            next(g)
            tails.append(g)

        pend = []
        for h in list(range(4)) + [None]:
            if h is not None:
                Es = []
                for mb in range(2):
                    Sc = ps_next()
                    mm(Sc[:, :], mkT[:, h, mb * 128:(mb + 1) * 128], cqT(h))
                    E = e_next()
                    act(E, Sc[:, :], AF.Exp, scale=CSCALE)
                    Es.append(E)
                pend.append((h, Es))
                tick()
            while len(pend) > 1 or (h is None and pend):
                ph, pEs = pend.pop(0)
                O = ps[(ph % 2) * 2]
                Ss = ps[(ph % 2) * 2 + 1]
                if ph >= 2:
                    drain()
                for mb in range(2):
                    mm(O[:, :], mvc[:, mb, ph * 128:(ph + 1) * 128], pEs[mb], start=(mb == 0), stop=(mb == 1))
                    mm(Ss[:, :], C(CM_ONES), pEs[mb], start=(mb == 0), stop=(mb == 1))
                start_c_tail(ph)

        drain()
        ps_pool["banks"] = [0, 1, 2, 3, 4, 5, 6]
        br_slot = None
        for dc in range(8):
            slot_g = use_unit(f"gate{dc}")
            if dc % 2 == 0:
                br_slot = use_unit(f"br{dc // 2}")
            gts = []
            for n in range(3):
                pg = ps_next()
                for kc in range(8):
                    mm(pg[:, :], Wv(slot_g, (n * 8 + kc) * 128, 128), hT[:, kc, :],
                       start=(kc == 0), stop=(kc == 7))
                g = e_next()
                gf = f_next()
                act(gf, pg[:, :], AF.Exp, bias=smb[:, dc * 3 + n:dc * 3 + n + 1], scale=-1.0)
                act(gf, gf, AF.Ln, bias=one_ap)
                act(g, gf, AF.Exp, scale=-1.0)
                gts.append(g)
            pbs = []
            for n in range(3):
                pb = ps_next()
                for kc in range(4):
                    mm(pb[:, :], Wv(br_slot, (((dc % 2) * 3 + n) * 4 + kc) * 128, 128), oT(n, kc),
                       start=(kc == 0), stop=(kc == 3))
                pbs.append(pb)
            a = f_next()
            b = f_next()
            tt(a, gts[0], pbs[0][:, :], ALU.mult)
            tt(b, gts[1], pbs[1][:, :], ALU.mult)
            tt(a, a, b, ALU.add)
            tt(b, gts[2], pbs[2][:, :], ALU.mult)
            tt(yT(dc), a, b, ALU.add)
        for g2 in range(2):
            slot = use_unit(f"wo{g2}")
            for dl in range(4):
                dc = g2 * 4 + dl
                po = ps_next()
                for kc in range(8):
                    mm(po[:, :], Wv(slot, (dl * 8 + kc) * 128, 128), yT(kc), start=(kc == 0), stop=(kc == 7))
                tt(x_sb[:, dc, :], po[:, :], x_sb[:, dc, :], ALU.add)
                if dc >= 1:
                    fused_sq(dc - 1)
        fused_sq(7)

        finish_norm(PV_FFN2, lambda kc: hT[:, kc, :])
        nxt = t + 1 if t + 1 < NT else None
        ffn(2, prefetch_tile=nxt)
        finish_norm(PV_FIN, lambda kc: Uf[:, kc, :], r=rnext[:, :])
        if nxt is not None:
            for k2 in range(3):
                dma(BQ, x_sb[:, 2 * k2:2 * k2 + 2, :], ftmp[:, 2 * k2:2 * k2 + 2, :], f"xc{k2}",
                    reads=[ftmp[:, 2 * k2:2 * k2 + 2, :]], writes=[x_sb[:, 2 * k2:2 * k2 + 2, :]])
            dma(BQ, x_sb[:, 6, :], cos_sb[:, :], "xc3", reads=[cos_sb[:, :]], writes=[x_sb[:, 6, :]])
            dma(BQ, x_sb[:, 7, :], sin_sb[:, :], "xc4", reads=[sin_sb[:, :]], writes=[x_sb[:, 7, :]])
        dma(BQ, outT.rearrange("(k p) s -> p k s", p=128)[:, :, c0t:c0t + T], Uf[:, :, :], "st",
            reads=[Uf[:, :, :]], writes=[("out", t)])

    sc.add("sp", None, reads=[("out", t) for t in range(NT)], writes=[])
    assert st["next_use"] == len(stream)
    sc.emit(nc, es)
    es.close()
    return nc


def make_core_inputs(inputs, b, S=SEQ, shared=None):
    x = np.asarray(inputs["x"])
    mem = np.asarray(inputs["mem"])
    d = dict(shared)
    d["xT"] = np.ascontiguousarray(x[b, :S].T)
    d["memT"] = np.ascontiguousarray(mem[b].T)
    return d


def make_shared(inputs, S=SEQ):
    inp = {k: np.asarray(v, dtype=np.float32) for k, v in inputs.items() if k not in ("x", "mem")}
    cmat, cosT, sinT = build_consts(S)
    return {
        "wall": build_wall(inp),
        "pvec": build_pvec(inp),
        "cmat": cmat,
        "cosT": cosT,
        "sinT": sinT,
        "btab": build_btab(inp["b_rel_bias"][0]),
    }


def kernel(**inputs):
    x = np.asarray(inputs["x"])
    B, S, _ = x.shape
    shared = make_shared(inputs, S)
    nc = build_program(S)
    in_maps = [make_core_inputs(inputs, b, S, shared) for b in range(B)]
    res = run_bass_kernel_spmd(nc, in_maps, core_ids=list(range(B)))
    out = np.empty((B, S, D), np.float32)
    for b in range(B):
        out[b] = np.asarray(res.results[b]["outT"]).T
    return out
```
